# Optimizing a Trainium2 kernel written in Bass

```python
import math
import jax, jax.numpy as jnp
from jax import lax
import numpy as np

D_MODEL = 4096
BATCH = 2
SEQ = 8192
DEPTH = 2

HEAD_DIM = 128
N_TOTAL_HEADS = D_MODEL // HEAD_DIM
GRID_W = 64
N_MEM = 256
A_Q_HEADS = 3 * N_TOTAL_HEADS // 8
A_KV_HEADS = A_Q_HEADS // 3
A_WINDOW = 128
A_BLOCK = 128
B_HEADS = N_TOTAL_HEADS // 4
B_WIN_ROWS = 8
B_WIN_COLS = 16
B_QCOLS = 16
B_KCOLS = 32
C_HEADS = N_TOTAL_HEADS - A_Q_HEADS - B_HEADS
C_Q_RANK = 1024
C_KV_RANK = 512
C_NOPE = 128
C_ROPE = 64
C_V = HEAD_DIM
C_BLOCK = 128
ROPE_BASE = 10000.0
X_HEADS = 4
X_DIM = X_HEADS * HEAD_DIM
D_FF = -(-8 * D_MODEL // 768) * 256

IN_WIDTHS = (A_Q_HEADS * HEAD_DIM, A_KV_HEADS * HEAD_DIM, A_KV_HEADS * HEAD_DIM,
             B_HEADS * HEAD_DIM, B_HEADS * HEAD_DIM, B_HEADS * HEAD_DIM,
             C_Q_RANK, C_KV_RANK, C_ROPE)
D_IN = sum(IN_WIDTHS)
MIX_W = (A_Q_HEADS + B_HEADS + C_HEADS) * HEAD_DIM
NORM_EPS = 1e-6
NEG_INF = -1e30

kernel_name = 'hybrid_parallel_heads_encoder'


def rms_norm(x, g):
    xf = x.astype(jnp.float32)
    y = xf * lax.rsqrt(jnp.mean(xf * xf, axis=-1, keepdims=True) + NORM_EPS)
    return (y * g.astype(jnp.float32)).astype(x.dtype)


def alibi_slopes(n):
    return 2.0 ** (-8.0 * jnp.arange(1, n + 1, dtype=jnp.float32) / n)


def rope_tables(s):
    inv = 1.0 / (ROPE_BASE ** (jnp.arange(0, C_ROPE, 2, dtype=jnp.float32) / C_ROPE))
    ang = jnp.arange(s, dtype=jnp.float32)[:, None] * inv[None, :]
    return jnp.cos(ang), jnp.sin(ang)


def apply_rope(x, cos, sin):
    x1, x2 = jnp.split(x.astype(jnp.float32), 2, axis=-1)
    c = cos[None, :, None, :]
    sn = sin[None, :, None, :]
    return jnp.concatenate([x1 * c - x2 * sn, x1 * sn + x2 * c], axis=-1).astype(x.dtype)


def windowed_gqa_alibi(q, k, v, sink):
    b, s, hq, d = q.shape
    hkv = k.shape[2]
    rep = hq // hkv
    nb = s // A_BLOCK
    qb = q.reshape(b, nb, A_BLOCK, hkv, rep, d)

    def band(t):
        tp = jnp.pad(t, ((0, 0), (A_BLOCK, A_BLOCK), (0, 0), (0, 0))).reshape(b, nb + 2, A_BLOCK, hkv, d)
        return jnp.concatenate([tp[:, :-2], tp[:, 1:-1], tp[:, 2:]], axis=2)

    kb, vb = band(k), band(v)
    scores = jnp.einsum('bnqgrd,bnkgd->bngrqk', qb, kb).astype(jnp.float32) * (1.0 / math.sqrt(d))
    qi = jnp.arange(A_BLOCK)[:, None]
    kj = jnp.arange(3 * A_BLOCK)[None, :]
    rel = kj - A_BLOCK - qi
    kpos = jnp.arange(nb)[:, None, None] * A_BLOCK - A_BLOCK + kj[None]
    valid = (jnp.abs(rel) <= A_WINDOW)[None] & (kpos >= 0) & (kpos < s)
    slopes = alibi_slopes(hq).reshape(hkv, rep)
    scores = scores - slopes[None, None, :, :, None, None] * jnp.abs(rel).astype(jnp.float32)
    scores = jnp.where(valid[None, :, None, None], scores, NEG_INF)
    sink_l = sink.astype(jnp.float32).reshape(hkv, rep)[None, None, :, :, None, None]
    m = jnp.maximum(jnp.max(scores, axis=-1, keepdims=True), sink_l)
    p = jnp.exp(scores - m)
    denom = jnp.sum(p, axis=-1, keepdims=True) + jnp.exp(sink_l - m)
    out = jnp.einsum('bngrqk,bnkgd->bnqgrd', (p / denom).astype(v.dtype), vb)
    return out.reshape(b, s, hq * d)


def neighbourhood_attn_2d(q, k, v, rpb):
    b, s, h, d = q.shape
    rows = s // GRID_W
    wr = min(B_WIN_ROWS, rows)
    qg = q.reshape(b, rows, GRID_W, h, d)
    kg = k.reshape(b, rows, GRID_W, h, d)
    vg = v.reshape(b, rows, GRID_W, h, d)
    nj = GRID_W // B_QCOLS
    qcol = np.arange(GRID_W).reshape(nj, B_QCOLS)
    kstart = np.clip(np.arange(nj) * B_QCOLS - B_WIN_COLS // 2, 0, GRID_W - B_KCOLS)
    kcol = kstart[:, None] + np.arange(B_KCOLS)[None, :]
    cstart = np.clip(qcol - B_WIN_COLS // 2, 0, GRID_W - B_WIN_COLS)
    col_mask = (kcol[:, None, :] >= cstart[..., None]) & (kcol[:, None, :] < cstart[..., None] + B_WIN_COLS)
    dc_idx = np.clip(kcol[:, None, :] - qcol[..., None] + B_WIN_COLS - 1, 0, 2 * B_WIN_COLS - 2)
    rpb_c = rpb.astype(jnp.float32)[:, :, dc_idx]
    col_mask = jnp.asarray(col_mask)
    scale = 1.0 / math.sqrt(d)

    def row_block(r):
        rs = jnp.clip(r - wr // 2, 0, rows - wr)
        qr = lax.dynamic_index_in_dim(qg, r, axis=1, keepdims=False).reshape(b, nj, B_QCOLS, h, d)
        kr = lax.dynamic_slice_in_dim(kg, rs, wr, axis=1)[:, :, kcol]
        vr = lax.dynamic_slice_in_dim(vg, rs, wr, axis=1)[:, :, kcol]
        sc = jnp.einsum('bjqhd,bwjchd->bhjqwc', qr, kr).astype(jnp.float32) * scale
        dr_idx = rs + jnp.arange(wr) - r + B_WIN_ROWS - 1
        bias = jnp.take(rpb_c, dr_idx, axis=1)
        sc = sc + bias.transpose(0, 2, 3, 1, 4)[None]
        sc = jnp.where(col_mask[None, None, :, :, None, :], sc, NEG_INF)
        p = jax.nn.softmax(sc.reshape(b, h, nj, B_QCOLS, wr * B_KCOLS), axis=-1)
        p = p.reshape(b, h, nj, B_QCOLS, wr, B_KCOLS).astype(vr.dtype)
        out = jnp.einsum('bhjqwc,bwjchd->bjqhd', p, vr)
        return out.reshape(b, GRID_W, h * d)

    out = lax.map(row_block, jnp.arange(rows))
    return out.transpose(1, 0, 2, 3).reshape(b, s, h * d)


def mla(c_q, c_kv, k_rope_in, q_norm, w_q_b, kv_norm, w_kv_b, cos, sin):
    b, s, _ = c_q.shape
    q = (rms_norm(c_q, q_norm) @ w_q_b).reshape(b, s, C_HEADS, C_NOPE + C_ROPE)
    q_nope, q_rope = q[..., :C_NOPE], apply_rope(q[..., C_NOPE:], cos, sin)
    kv = (rms_norm(c_kv, kv_norm) @ w_kv_b).reshape(b, s, C_HEADS, C_NOPE + C_V)
    k_nope, vv = kv[..., :C_NOPE], kv[..., C_NOPE:]
    k_rope = apply_rope(k_rope_in[:, :, None, :], cos, sin)[:, :, 0]
    scale = 1.0 / math.sqrt(C_NOPE + C_ROPE)
    nb = s // C_BLOCK
    qn_b = q_nope.reshape(b, nb, C_BLOCK, C_HEADS, C_NOPE).transpose(1, 0, 2, 3, 4)
    qr_b = q_rope.reshape(b, nb, C_BLOCK, C_HEADS, C_ROPE).transpose(1, 0, 2, 3, 4)

    def blk(args):
        qn, qr = args
        sc = (jnp.einsum('bqhd,bkhd->bhqk', qn, k_nope).astype(jnp.float32)
              + jnp.einsum('bqhr,bkr->bhqk', qr, k_rope).astype(jnp.float32)) * scale
        p = jax.nn.softmax(sc, axis=-1).astype(vv.dtype)
        return jnp.einsum('bhqk,bkhd->bqhd', p, vv)

    out = lax.map(blk, (qn_b, qr_b))
    return out.transpose(1, 0, 2, 3, 4).reshape(b, s, C_HEADS * C_V)


def memory_xattn(h, mem_n, w_q, w_k, w_v, w_o):
    b, s, _ = h.shape
    m = mem_n.shape[1]
    q = (h @ w_q).reshape(b, s, X_HEADS, HEAD_DIM)
    k = (mem_n @ w_k).reshape(b, m, X_HEADS, HEAD_DIM)
    v = (mem_n @ w_v).reshape(b, m, X_HEADS, HEAD_DIM)
    sc = jnp.einsum('bshd,bmhd->bhsm', q, k).astype(jnp.float32) * (1.0 / math.sqrt(HEAD_DIM))
    p = jax.nn.softmax(sc, axis=-1).astype(v.dtype)
    o = jnp.einsum('bhsm,bmhd->bshd', p, v).reshape(b, s, X_DIM)
    return o @ w_o


def swiglu(h, w_gate, w_up, w_down):
    return (jax.nn.silu(h @ w_gate) * (h @ w_up)) @ w_down


def _w(key, shape, fan_in):
    return jax.random.normal(key, shape, jnp.float32) * (fan_in ** -0.5)


def _gain(key, shape):
    return 1.0 + 0.02 * jax.random.normal(key, shape, jnp.float32)


def setup_inputs(seed: int = 0) -> dict:
    key = jax.random.key(seed)
    ks = jax.random.split(key, 24)
    L = DEPTH
    return {
        'x': jax.random.normal(ks[0], (BATCH, SEQ, D_MODEL), jnp.float32),
        'mem': jax.random.normal(ks[1], (BATCH, N_MEM, D_MODEL), jnp.float32),
        'ln_mix': _gain(ks[2], (L, D_MODEL)),
        'w_in': _w(ks[3], (L, D_MODEL, D_IN), D_MODEL),
        'a_sink': 0.5 * jax.random.normal(ks[4], (L, A_Q_HEADS), jnp.float32),
        'b_rpb': 0.1 * jax.random.normal(ks[5], (L, B_HEADS, 2 * B_WIN_ROWS - 1, 2 * B_WIN_COLS - 1), jnp.float32),
        'c_q_norm': _gain(ks[6], (L, C_Q_RANK)),
        'c_w_q_b': _w(ks[7], (L, C_Q_RANK, C_HEADS * (C_NOPE + C_ROPE)), C_Q_RANK),
        'c_kv_norm': _gain(ks[8], (L, C_KV_RANK)),
        'c_w_kv_b': _w(ks[9], (L, C_KV_RANK, C_HEADS * (C_NOPE + C_V)), C_KV_RANK),
        'w_out': _w(ks[10], (L, MIX_W, D_MODEL), MIX_W),
        'ln_xattn': _gain(ks[11], (L, D_MODEL)),
        'ln_mem': _gain(ks[12], (L, D_MODEL)),
        'x_w_q': _w(ks[13], (L, D_MODEL, X_DIM), D_MODEL),
        'x_w_k': _w(ks[14], (L, D_MODEL, X_DIM), D_MODEL),
        'x_w_v': _w(ks[15], (L, D_MODEL, X_DIM), D_MODEL),
        'x_w_o': _w(ks[16], (L, X_DIM, D_MODEL), X_DIM),
        'ln_ffn': _gain(ks[17], (L, D_MODEL)),
        'w_gate': _w(ks[18], (L, D_MODEL, D_FF), D_MODEL),
        'w_up': _w(ks[19], (L, D_MODEL, D_FF), D_MODEL),
        'w_down': _w(ks[20], (L, D_FF, D_MODEL), D_FF),
        'ln_final': _gain(ks[21], (D_MODEL,)),
    }


def reference(x, mem, ln_mix, w_in, a_sink, b_rpb, c_q_norm, c_w_q_b, c_kv_norm, c_w_kv_b, w_out,
              ln_xattn, ln_mem, x_w_q, x_w_k, x_w_v, x_w_o, ln_ffn, w_gate, w_up, w_down, ln_final):
    b, s, _ = x.shape
    cos, sin = rope_tables(s)
    split_at = np.cumsum(IN_WIDTHS)[:-1].tolist()
    for l in range(DEPTH):
        h = rms_norm(x, ln_mix[l])
        aq, ak, av, bq, bk, bv, cq, ckv, ckr = jnp.split(h @ w_in[l], split_at, axis=-1)
        ya = windowed_gqa_alibi(aq.reshape(b, s, A_Q_HEADS, HEAD_DIM),
                                ak.reshape(b, s, A_KV_HEADS, HEAD_DIM),
                                av.reshape(b, s, A_KV_HEADS, HEAD_DIM), a_sink[l])
        yb = neighbourhood_attn_2d(bq.reshape(b, s, B_HEADS, HEAD_DIM),
                                   bk.reshape(b, s, B_HEADS, HEAD_DIM),
                                   bv.reshape(b, s, B_HEADS, HEAD_DIM), b_rpb[l])
        yc = mla(cq, ckv, ckr, c_q_norm[l], c_w_q_b[l], c_kv_norm[l], c_w_kv_b[l], cos, sin)
        x = x + jnp.concatenate([ya, yb, yc], axis=-1) @ w_out[l]
        x = x + memory_xattn(rms_norm(x, ln_xattn[l]), rms_norm(mem, ln_mem[l]),
                             x_w_q[l], x_w_k[l], x_w_v[l], x_w_o[l])
        x = x + swiglu(rms_norm(x, ln_ffn[l]), w_gate[l], w_up[l], w_down[l])
    return rms_norm(x, ln_final)
```

```python
import math
import numpy as np
import ml_dtypes
import concourse.bass as bass
import concourse.mybir as mybir
from concourse.bass_utils import run_bass_kernel_spmd

F32 = mybir.dt.float32
BF16 = mybir.dt.bfloat16
ALU = mybir.AluOpType
AF = mybir.ActivationFunctionType

D = 4096
DC = 32
DFF = 11008
FC = 86
FH = 43
D_IN = 7232
NEG = -30000.0
EPS = 1e-6
WSLOT = 5504


class KeyState:
    __slots__ = ("wr_op", "wr_dma", "rd_eng", "rd_dma")

    def __init__(self):
        self.wr_op = None
        self.wr_dma = {}
        self.rd_eng = {}
        self.rd_dma = {}


class Prog:
    ENGS = ("pe", "act", "dve", "pool", "sp")

    def __init__(self):
        self.ops = []
        self.keys = {}
        self.dma_cnt = {}
        self.epoch = 0

    def _ks(self, k):
        s = self.keys.get(k)
        if s is None:
            s = self.keys[k] = KeyState()
        return s

    def _add(self, eng, fn, reads, writes, dsem=None, multi=False, dinc=16):
        op = dict(eng=eng, fn=fn, deps_op=set(), deps_dma={}, dsem=dsem, sig=False, epoch=self.epoch,
                  id=len(self.ops), dinc=dinc)
        if dsem is not None:
            c = self.dma_cnt.get(dsem, 0) + dinc
            self.dma_cnt[dsem] = c
            op["dcount"] = c
        for k in reads:
            s = self._ks(k)
            if s.wr_op is not None:
                op["deps_op"].add(s.wr_op)
            for sm, c in s.wr_dma.items():
                op["deps_dma"][sm] = max(op["deps_dma"].get(sm, 0), c)
        for k in writes:
            s = self._ks(k)
            if s.wr_op is not None:
                op["deps_op"].add(s.wr_op)
            if not (multi and dsem is not None):
                for sm, c in s.wr_dma.items():
                    op["deps_dma"][sm] = max(op["deps_dma"].get(sm, 0), c)
            for e, oid in s.rd_eng.items():
                op["deps_op"].add(oid)
            for sm, c in s.rd_dma.items():
                op["deps_dma"][sm] = max(op["deps_dma"].get(sm, 0), c)
        for k in reads:
            s = self._ks(k)
            if dsem is None:
                s.rd_eng[eng] = op["id"]
            else:
                s.rd_dma[dsem] = op["dcount"]
        for k in writes:
            s = self._ks(k)
            s.rd_eng = {}
            s.rd_dma = {}
            if dsem is None:
                s.wr_op = op["id"]
                s.wr_dma = {}
            else:
                s.wr_op = None
                if not multi:
                    s.wr_dma = {}
                s.wr_dma[dsem] = op["dcount"]
        if dsem is not None:
            op["deps_dma"].pop(dsem, None) if False else None
        self.ops.append(op)
        return op

    def op(self, eng, fn, reads=(), writes=()):
        return self._add(eng, fn, reads, writes)

    def dma(self, q, out, in_, reads, writes, sem, multi=False):
        return self._add(q, lambda e, o=out, i=in_: e.dma_start(out=o, in_=i), reads, writes, dsem=sem, multi=multi)

    def cc(self, fn, reads, writes, sem):
        return self._add("pool", fn, reads, writes, dsem=sem, dinc=1)

    def finalize_and_emit(self, nc, es):
        ops = self.ops
        for op in ops:
            for pid in op["deps_op"]:
                p = ops[pid]
                if p["eng"] == op["eng"] and op["eng"] == "pe":
                    continue
                p["sig"] = True
        cnt = {}
        for op in ops:
            if op["sig"]:
                k = (op["eng"], op["epoch"])
                cnt[k] = cnt.get(k, 0) + 1
                op["sigidx"] = cnt[k]
        sems = {}

        def getsem(k):
            if k not in sems:
                sems[k] = es.enter_context(nc.semaphore("s%d" % len(sems)))
            return sems[k]

        for op in ops:
            w = {}
            for pid in op["deps_op"]:
                p = ops[pid]
                if p["eng"] == op["eng"] and op["eng"] == "pe":
                    continue
                k = ("E", p["eng"], p["epoch"])
                w[k] = max(w.get(k, 0), p["sigidx"])
            for sm, c in op["deps_dma"].items():
                k = ("D", sm)
                w[k] = max(w.get(k, 0), c)
            op["waits"] = w
        for k in list({kk for op in ops for kk in op["waits"]}):
            getsem(k)
        for op in ops:
            if op["sig"]:
                getsem(("E", op["eng"], op["epoch"]))
            if op["dsem"] is not None:
                getsem(("D", op["dsem"]))
        per = {e: [o for o in ops if o["eng"] == e] for e in self.ENGS}
        self.nsem = len(sems)

        def run(e, handle):
            waited = {}
            for op in per[e]:
                for k, v in op["waits"].items():
                    if waited.get(k, 0) < v:
                        handle.wait_ge(sems[k], v)
                        waited[k] = v
                ins = op["fn"](handle)
                if op["dsem"] is not None:
                    if op["dinc"] == 16:
                        ins.then_inc(sems[("D", op["dsem"])], 16)
                    else:
                        ins.then_inc(sems[("D", op["dsem"])])
                elif op["sig"]:
                    ins.then_inc(sems[("E", e, op["epoch"])], 1)
            for op in per[e]:
                pass
            last = {}
            for op in per[e]:
                if op["dsem"] is not None:
                    last[("D", op["dsem"])] = op["dcount"]
            for k, v in last.items():
                if waited.get(k, 0) < v:
                    handle.wait_ge(sems[k], v)

        with nc.Block() as block:
            @block.tensor
            def _(h):
                run("pe", h)

            @block.scalar
            def _(h):
                run("act", h)

            @block.vector
            def _(h):
                run("dve", h)

            @block.gpsimd
            def _(h):
                run("pool", h)

            @block.sync
            def _(h):
                run("sp", h)


def build(S, L, taps=(), stop=None, groups=((0, 1, 2, 3), (4, 5, 6, 7))):
    from contextlib import ExitStack
    nc = bass.Bass("TRN2", target_bir_lowering=False)
    NT = S // 512
    NB = S // 128
    SK = 4 * S
    NBK = SK // 128
    NU = 17
    P = Prog()
    es = ExitStack()

    def din(name, shape, dt=F32):
        return nc.dram_tensor(name, list(shape), dt, kind="ExternalInput").ap()

    def dscr(name, shape, dt=BF16):
        return nc.dram_tensor(name, list(shape), dt).ap()

    x_in = din("x", [S, D])
    mem_in = din("mem", [256, D])
    w_in_d = din("w_in", [L, D, D_IN])
    wqb_d = din("c_w_q_b", [L, 1024, 2304])
    wkvb_d = din("c_w_kv_b", [L, 512, 3072])
    wout_d = din("w_out", [L, D, D])
    xwq_d = din("x_w_q", [L, D, 512])
    xwk_d = din("x_w_k", [L, D, 512])
    xwv_d = din("x_w_v", [L, D, 512])
    xwo_d = din("x_w_o", [L, 512, D])
    wg_d = din("w_gate", [L, D, DFF])
    wu_d = din("w_up", [L, D, DFF])
    wd_d = din("w_down", [L, DFF, D])
    NG = L * 3 * 32 + 32
    gains_d = din("gains", [128, NG])
    gmem_d = din("gmem", [128, L * 32])
    gq_d = din("gq", [128, L * 8])
    gkv_d = din("gkv", [128, L * 4])
    sink_d = din("sink", [128, L * 12])
    ident_d = din("ident", [128, 128])
    sel_d = din("sel", [128, 4])
    edge_d = din("edge", [128, 2])
    cos_d = din("costab", [64, S])
    sin_d = din("sintab", [64, S])
    biasA_d = din("biasA", [128, 12 * 384])
    bT_d = din("bT", [L * 8, 128, 896])
    maskB_d = din("maskB", [128, 5 * 896], BF16)
    out_d = nc.dram_tensor("out", [S, D], F32, kind="ExternalOutput").ap()
    tap_d = {}
    for nm, shp, dt in taps:
        tap_d[nm] = nc.dram_tensor("tap_" + nm, list(shp), dt, kind="ExternalOutput").ap()

    xT_d = dscr("xT_s", [DC, 128, S], F32)
    Wc = {}
    for l in range(L):
        Wc[("in", l)] = dscr("wc_in%d" % l, [57, 128, 32 * 128])
        Wc[("qb", l)] = dscr("wc_qb%d" % l, [24, 128, 8 * 128])
        Wc[("kvb", l)] = dscr("wc_kvb%d" % l, [24, 128, 4 * 128])
        Wc[("out", l)] = dscr("wc_out%d" % l, [32, 128, 32 * 128])
        Wc[("xq", l)] = dscr("wc_xq%d" % l, [4, 128, 32 * 128])
        Wc[("xk", l)] = dscr("wc_xk%d" % l, [4, 128, 32 * 128])
        Wc[("xv", l)] = dscr("wc_xv%d" % l, [4, 128, 32 * 128])
        Wc[("xo", l)] = dscr("wc_xo%d" % l, [32, 128, 4 * 128])
        Wc[("g", l)] = dscr("wc_g%d" % l, [FC, 128, 32 * 128])
        Wc[("u", l)] = dscr("wc_u%d" % l, [FC, 128, 32 * 128])
        Wc[("d", l)] = dscr("wc_d%d" % l, [32, 128, FC * 128])
    qA_d = dscr("qA_s", [12, 128, S])
    qB_d = dscr("qB_s", [8, 128, S])
    qCn_d = dscr("qCn_s", [12, 128, S])
    qCr_d = dscr("qCr_s", [12, 64, S])
    kA_d = dscr("kA_s", [4, 128, S])
    kB_d = dscr("kB_s", [8, 128, S])
    kCn_d = dscr("kCn_s", [12, 128, S])
    kCr_d = dscr("kCr_s", [64, S])
    vA_d = dscr("vA_s", [4, 128, S])
    vB_d = dscr("vB_s", [8, 128, S])
    vC_d = dscr("vC_s", [12, 128, S])
    cat_d = dscr("cat_s", [32, 128, S])
    XS_d = dscr("XS_s", [4 * NU * 128, 4096])
    XR_d = dscr("XR_s", [4 * NU * 128, 4096])
    hA_d = dscr("hA_s", [4, 128, 512])
    hB_d = dscr("hB_s", [4, 128, 3072])

    def sb(name, shape, dt):
        return es.enter_context(nc.sbuf_tensor(name, list(shape), dt))

    def ps(name, shape):
        return es.enter_context(nc.psum_tensor(name, list(shape), F32))

    xT = sb("xT", [128, DC, 512], F32)
    hT = sb("hT", [128, DC, 512], BF16)
    big = sb("big", [128, FH * 512], BF16)
    wsl = [sb("wsl%d" % i, [128, 4096], BF16) for i in range(3)]
    consts = sb("consts", [128, NG + L * 32 + L * 8 + L * 4 + L * 12 + L * 12 + 8], F32)
    ident = sb("ident_sb", [128, 128], F32)
    ones = sb("ones", [128, 128], BF16)
    rstd = sb("rstd", [128, 512], F32)
    tmpf = [sb("tmpf%d" % i, [128, 896], F32) for i in range(2)]
    stg = [sb("stg%d" % i, [128, 896], BF16) for i in range(3)]
    rcp2 = sb("rcp2", [128, 1024], F32)
    cs_t = rcp2[0:64, 0:512]
    sn_t = rcp2[0:64, 512:1024]
    small = sb("small", [128, 6144], BF16)
    bias_sb = sb("bias_sb", [128, 896], F32)
    maskB = big[:, 16384:16384 + 5 * 896]
    psb = [ps("psb%d" % i, [128, 512]) for i in range(8)]
    XTK = ["xT0", "xT1", "xT2", "xT3"]

    def RK(buf, lo, hi):
        gran = {"xT": 4096, "hT": 8192, "big": 8192, "sm": 1024}[buf]
        return ["%s%d" % (buf, i) for i in range(lo // gran, (hi - 1) // gran + 1)]

    def xk(c):
        return "xT%d" % (c // 8)

    def hk(c):
        return "hT%d" % (c // 16)

    o_g = 0
    o_gmem = NG
    o_gq = o_gmem + L * 32
    o_gkv = o_gq + L * 8
    o_sink = o_gkv + L * 4
    o_esink = o_sink + L * 12
    o_sel = o_esink + L * 12
    o_edge = o_sel + 4

    def DMA(out, in_, reads, writes, sem, q="sp", multi=False):
        P.dma(q, out, in_, reads, writes, sem, multi)

    def MM(out, lhsT, rhs, start, stop, reads, writes):
        P.op("pe", lambda e, o=out, a=lhsT, b=rhs, s=start, t=stop: e.matmul(o, a, b, start=s, stop=t), reads, writes)

    def ACT(out, in_, func, reads, writes, scale=1.0):
        P.op("act", lambda e, o=out, i=in_, f=func, s=scale: e.activation(o, i, f, scale=s), reads, writes)

    def TS(eng, out, in0, s1, s2, op0, op1, reads, writes):
        if s2 is None:
            P.op(eng, lambda e, o=out, i=in0, a=s1, p=op0: e.tensor_scalar(o, i, a, None, p), reads, writes)
        else:
            P.op(eng, lambda e, o=out, i=in0, a=s1, b=s2, p=op0, q=op1: e.tensor_scalar(o, i, a, b, p, q), reads, writes)

    def STT(eng, out, in0, sc, in1, op0, op1, reads, writes):
        P.op(eng, lambda e, o=out, i=in0, s=sc, j=in1, p=op0, q=op1: e.scalar_tensor_tensor(o, i, s, j, p, q), reads, writes)

    def TT(eng, out, in0, in1, opx, reads, writes):
        P.op(eng, lambda e, o=out, i=in0, j=in1, p=opx: e.tensor_tensor(o, i, j, p), reads, writes)

    def CP(eng, out, in_, reads, writes):
        if eng == "act":
            P.op("act", lambda e, o=out, i=in_: e.activation(o, i, AF.Copy), reads, writes)
        else:
            P.op(eng, lambda e, o=out, i=in_: e.tensor_copy(o, i), reads, writes)

    rr = {"ps": 0, "ps3": 0, "ws": 0, "stg": 0, "tmp": 0, "ev": 0, "ost": 0, "kv": 0, "q": 0}

    def nxt(name, n):
        v = rr[name]
        rr[name] = (v + 1) % n
        return v

    DMA(consts[:, o_g:o_g + NG], gains_d[:, :], [], ["consts"], "c0")
    DMA(consts[:, o_gmem:o_gmem + L * 32], gmem_d[:, :], [], ["consts"], "c0", multi=True)
    DMA(consts[:, o_gq:o_gq + L * 8], gq_d[:, :], [], ["consts"], "c0", multi=True)
    DMA(consts[:, o_gkv:o_gkv + L * 4], gkv_d[:, :], [], ["consts"], "c0", multi=True)
    DMA(consts[:, o_sink:o_sink + L * 12], sink_d[:, :], [], ["consts"], "c0", multi=True)
    DMA(ident[:, :], ident_d[:, :], [], ["ident"], "c1")
    DMA(consts[:, o_sel:o_sel + 4], sel_d[:, :], [], ["consts"], "c0", multi=True)
    DMA(consts[:, o_edge:o_edge + 2], edge_d[:, :], [], ["consts"], "c0", multi=True)
    P.op("pool", lambda e: e.memset(ones[:, :], 1.0), [], ["ones"])
    ACT(consts[:, o_esink:o_esink + L * 12], consts[:, o_sink:o_sink + L * 12], AF.Exp, ["consts"], ["esink"])

    xflat = xT[:, :, :].rearrange("p a b -> p (a b)")
    hflat = hT[:, :, :].rearrange("p a b -> p (a b)")

    NFG = 6
    fgb = [xflat[:, i * 1024:(i + 1) * 1024] for i in range(NFG)]
    bgb = [sb("bgb%d" % i, [128, 1024], F32) for i in range(2)]

    def reg_tasks(W2d, K, N, dstWc, key, out):
        KC = K // 128
        for c in range(N // 128):
            for k0 in range(0, KC, 8):
                kg = min(8, KC - k0)
                out.append(([(W2d[k0 * 128:(k0 + kg) * 128, c * 128:(c + 1) * 128], 0, 128)], kg, dstWc[c][:, k0 * 128:(k0 + kg) * 128], key))

    def special_tasks(l, out):
        key = "W%d" % l
        w = w_in_d[l]
        for k0 in range(0, 32, 8):
            rows = slice(k0 * 128, (k0 + 8) * 128)
            out.append(([(w[rows, 7168:7232], 0, 64), (w[rows, 7200:7232], 64, 32), (w[rows, 7168:7200], 96, 32)], 8,
                        Wc[("in", l)][56][:, k0 * 128:(k0 + 8) * 128], key))
        q = wqb_d[l]
        for h in range(12):
            b0 = h * 192
            out.append(([(q[:, b0:b0 + 128], 0, 128)], 8, Wc[("qb", l)][2 * h][:, :], key))
            out.append(([(q[:, b0 + 128:b0 + 192], 0, 64), (q[:, b0 + 160:b0 + 192], 64, 32), (q[:, b0 + 128:b0 + 160], 96, 32)], 8,
                        Wc[("qb", l)][2 * h + 1][:, :], key))

    def first_tasks(l, out, key):
        reg_tasks(w_in_d[l][:, 0:7168], D, 7168, Wc[("in", l)], key, out)
        reg_tasks(wkvb_d[l], 512, 3072, Wc[("kvb", l)], key, out)

    def rest_tasks(l, out, key):
        reg_tasks(wout_d[l], D, D, Wc[("out", l)], key, out)
        reg_tasks(xwq_d[l], D, 512, Wc[("xq", l)], key, out)
        reg_tasks(xwk_d[l], D, 512, Wc[("xk", l)], key, out)
        reg_tasks(xwv_d[l], D, 512, Wc[("xv", l)], key, out)
        reg_tasks(xwo_d[l], 512, D, Wc[("xo", l)], key, out)
        reg_tasks(wg_d[l], D, DFF, Wc[("g", l)], key, out)
        reg_tasks(wu_d[l], D, DFF, Wc[("u", l)], key, out)
        reg_tasks(wd_d[l], DFF, D, Wc[("d", l)], key, out)

    def task_load(task, buf, bkey, sem, q="sp"):
        pieces, kg, dst, key = task
        view = buf[:, 0:kg * 128].rearrange("p (k n) -> p k n", k=kg)
        for i, (src, off, w) in enumerate(pieces):
            DMA(view[:, :, off:off + w], src.rearrange("(k p) n -> p k n", p=128), [], [bkey], sem, q=q, multi=(i > 0))

    def task_store(task, buf, bkey, sem):
        pieces, kg, dst, key = task
        DMA(dst, buf[:, 0:kg * 128], [bkey], [key], sem, q="pool", multi=True)

    fg_tasks = []
    for l in range(L):
        special_tasks(l, fg_tasks)
    first_tasks(0, fg_tasks, "W0")
    for i, tk in enumerate(fg_tasks):
        bi = i % NFG
        task_load(tk, fgb[bi], "fg%d" % bi, "Lfg%d" % bi)
        task_store(tk, fgb[bi], "fg%d" % bi, "Sfg%d" % bi)

    bg_tasks = []
    rest_tasks(0, bg_tasks, "W0b")
    for l in range(1, L):
        first_tasks(l, bg_tasks, "W%db" % l)
        rest_tasks(l, bg_tasks, "W%db" % l)
    bg_bufs = [bgb[0][:, :], bgb[1][:, :], big[:, 16384:18432].bitcast(F32), big[:, 18432:20480].bitcast(F32)]
    bg_state = {"i": 0, "nb": 4, "slot": 0, "pending": [], "phase": "A", "callsC": 0}

    def bg_emit_store():
        tk, bi = bg_state["pending"].pop(0)
        task_store(tk, bg_bufs[bi], "bg%d" % bi, "Sbg%d" % bi)

    def bg_step(n=1):
        for _ in range(n):
            i = bg_state["i"]
            nb = bg_state["nb"]
            if i < len(bg_tasks):
                bi = bg_state["slot"] % nb
                bg_state["slot"] += 1
                task_load(bg_tasks[i], bg_bufs[bi], "bg%d" % bi, "Lbg%d" % bi, q=("pool" if bg_state["phase"] == "C" else "sp"))
                bg_state["pending"].append((bg_tasks[i], bi))
                bg_state["i"] = i + 1
                while len(bg_state["pending"]) > nb - 1:
                    bg_emit_store()
            elif bg_state["pending"]:
                bg_emit_store()

    def bg_set_nb(nb):
        while bg_state["pending"]:
            bg_emit_store()
        bg_state["nb"] = nb
        bg_state["slot"] = 0

    def bg_flush():
        while bg_state["i"] < len(bg_tasks):
            bg_step()
        while bg_state["pending"]:
            bg_emit_store()

    def transpose_in():
        for b in range(NB):
            h2 = b % 2
            xin = xflat[:, h2 * 4096:(h2 + 1) * 4096]
            kin = ["xT%d" % h2]
            fgk = ["fg%d" % i for i in range(h2 * 4, h2 * 4 + 4)] if b < 2 else []
            DMA(xin, x_in[b * 128:(b + 1) * 128, :], [], kin + fgk, "Lxin%d" % h2)
            xo = xflat[:, 8192 + h2 * 4096:8192 + (h2 + 1) * 4096].rearrange("p (c j) -> p c j", c=32)
            ko = ["xT%d" % (2 + h2)]
            for c4 in range(8):
                pi = nxt("ps", 8)
                for j in range(4):
                    c = c4 * 4 + j
                    P.op("pe", lambda e, o=psb[pi][:, j * 128:(j + 1) * 128], i=xin[:, c * 128:(c + 1) * 128]: e.transpose(o, i, ident[:, :]),
                         kin + ["ident"], ["ps%d" % pi])
                eng = "dve" if c4 % 2 == 0 else "act"
                CP(eng, xo[:, c4 * 4:c4 * 4 + 4, :], psb[pi][:, :].rearrange("p (c j) -> p c j", c=4), ["ps%d" % pi], ko)
            DMA(xT_d[:, :, b * 128:(b + 1) * 128].rearrange("c p n -> p c n"), xo, ko, ["xTd"], "Sxo%d" % h2, q="pool", multi=True)

    transpose_in()

    def load_w(wc_ap, nel, l):
        si = nxt("ws", 3)
        wk = ["W%d" % l] if (l == 0 and bg_state["phase"] == "A") else ["W%d" % l, "W%db" % l]
        DMA(wsl[si][:, 0:nel], wc_ap, wk, ["ws%d" % si], "Lws%d_%d" % (si, l))
        if l == 0:
            if bg_state["phase"] == "A":
                bg_step(3)
            else:
                bg_state["callsC"] += 1
                rem_calls = max(1, 1224 - bg_state["callsC"])
                rem_tasks = len(bg_tasks) - bg_state["i"]
                bg_step(max(1, min(3, -(-rem_tasks // rem_calls))))
        return si

    def load_xT(tok, extra_reads=()):
        for qd in range(4):
            DMA(xT[:, qd * 8:(qd + 1) * 8, :], xT_d[qd * 8:(qd + 1) * 8, :, tok].rearrange("c p n -> p c n"),
                ["xTd"] + list(extra_reads), ["xT%d" % qd], "LxT%d" % qd)

    def rmsnorm(src_fn, src_key_fn, nchunk, dim, gain_off, dst_fn, dst_key_fn, ntok=512):
        pi = nxt("ps", 8)
        for c in range(nchunk):
            gi = nxt("stg", 3)
            ACT(stg[gi][:, 0:ntok], src_fn(c), AF.Square, [src_key_fn(c)], ["stg%d" % gi])
            MM(psb[pi][:, 0:ntok], ones[:, :], stg[gi][:, 0:ntok], c == 0, c == nchunk - 1, ["ones", "stg%d" % gi], ["ps%d" % pi])
        TS("dve", rstd[:, 0:ntok], psb[pi][:, 0:ntok], 1.0 / dim, EPS, ALU.mult, ALU.add, ["ps%d" % pi], ["rstd"])
        P.op("act", lambda e: e.activation(rstd[:, 0:ntok], rstd[:, 0:ntok], AF.Sqrt), ["rstd"], ["rstd"])
        P.op("dve", lambda e: e.reciprocal(rstd[:, 0:ntok], rstd[:, 0:ntok]), ["rstd"], ["rstd"])
        for c in range(nchunk):
            eng = "dve"
            STT(eng, dst_fn(c), src_fn(c), consts[:, gain_off + c:gain_off + c + 1], rstd[:, 0:ntok], ALU.mult, ALU.mult,
                [src_key_fn(c), "rstd", "consts"], [dst_key_fn(c)])

    def gemm_fm(wc, chunks, KC, rhs_fn, rhs_key_fn, evac, l, M=128, ntok=512, lo=0):
        for ci in chunks:
            si = load_w(wc[ci], KC * 128, l)
            pi = nxt("ps", 8)
            for kc in range(KC):
                MM(psb[pi][0:M, 0:ntok], wsl[si][:, kc * 128 + lo:kc * 128 + lo + M], rhs_fn(kc), kc == 0, kc == KC - 1,
                   ["ws%d" % si, rhs_key_fn(kc)], ["ps%d" % pi])
            evac(ci, pi)

    def gemm_tm(wc, ci, KC, lhs_fn, lhs_key_fn, l, nblk=4):
        si = load_w(wc[ci], KC * 128, l)
        pi = nxt("ps", 8)
        for tb in range(nblk):
            for kc in range(KC):
                MM(psb[pi][:, tb * 128:(tb + 1) * 128], lhs_fn(kc, tb), wsl[si][:, kc * 128:(kc + 1) * 128], kc == 0, kc == KC - 1,
                   ["ws%d" % si, lhs_key_fn(kc)], ["ps%d" % pi])
        return pi

    def evac_store(pi, dst, dkey, M=128, n=512):
        gi = nxt("stg", 3)
        eng = "act" if nxt("ev", 2) == 0 else "dve"
        CP(eng, stg[gi][0:M, 0:n], psb[pi][0:M, 0:n], ["ps%d" % pi], ["stg%d" % gi])
        DMA(dst, stg[gi][0:M, 0:n], ["stg%d" % gi], [dkey], "Sstg%d" % gi, q="pool", multi=True)

    def rope_store(pa, pb, dst, dkey):
        t0, t1 = tmpf[0], tmpf[1]
        TT("dve", t0[0:64, 0:512], psb[pa][0:64, :], cs_t, ALU.mult, ["ps%d" % pa, "cs"], ["tmp0"])
        TT("dve", t1[0:64, 0:512], psb[pb][0:64, :], sn_t, ALU.mult, ["ps%d" % pb, "cs"], ["tmp1"])
        gi = nxt("stg", 3)
        TT("dve", stg[gi][0:64, 0:512], t0[0:64, 0:512], t1[0:64, 0:512], ALU.add, ["tmp0", "tmp1"], ["stg%d" % gi])
        DMA(dst, stg[gi][0:64, 0:512], ["stg%d" % gi], [dkey], "Sstg%d" % gi, q="pool", multi=True)

    ncq = small[:, 0:4096].rearrange("p (c n) -> p c n", c=8)
    nckv = small[:, 4096:6144].rearrange("p (c n) -> p c n", c=4)
    cqT = big[:, 0:8192].bitcast(F32).rearrange("p (c n) -> p c n", c=8)
    ckvT = big[:, 8192:12288].bitcast(F32).rearrange("p (c n) -> p c n", c=4)
    smk = lambda c: "sm%d" % (c // 2)

    def phaseA(l, t):
        tok = slice(t * 512, (t + 1) * 512)
        load_xT(tok)
        DMA(cs_t, cos_d[:, tok], [], ["cs", "rc0", "rc1a", "rc1b"], "Lcs")
        DMA(sn_t, sin_d[:, tok], [], ["cs"], "Lcs", multi=True)
        rmsnorm(lambda c: xT[:, c, :], xk, 32, D, o_g + (l * 3 + 0) * 32, lambda c: hT[:, c, :], hk)
        win = Wc[("in", l)]
        rhs = lambda kc: hT[:, kc, :]

        def ev_q(dst3, base, key):
            return lambda ci, pi: evac_store(pi, dst3[ci - base, :, tok], key)

        gemm_fm(win, range(0, 12), 32, rhs, hk, ev_q(qA_d, 0, "q"), l)
        gemm_fm(win, range(12, 16), 32, rhs, hk, ev_q(kA_d, 12, "kv"), l)
        gemm_fm(win, range(20, 28), 32, rhs, hk, ev_q(qB_d, 20, "q"), l)
        gemm_fm(win, range(28, 36), 32, rhs, hk, ev_q(kB_d, 28, "kv"), l)

        def ev_c(dstT, base, key):
            def f(ci, pi):
                CP("act", dstT[:, ci - base, :], psb[pi][:, :], ["ps%d" % pi], [key])
            return f

        gemm_fm(win, range(44, 52), 32, rhs, hk, ev_c(cqT, 44, "big0"), l)
        gemm_fm(win, range(52, 56), 32, rhs, hk, ev_c(ckvT, 52, "big1"), l)
        pp = []
        gemm_fm(win, [56], 32, rhs, hk, lambda ci, pi: pp.append(pi), l, M=64, lo=0)
        gemm_fm(win, [56], 32, rhs, hk, lambda ci, pi: pp.append(pi), l, M=64, lo=64)
        rope_store(pp[0], pp[1], kCr_d[:, tok], "kv")
        lhs = lambda kc, tb: hT[:, kc, tb * 128:(tb + 1) * 128]
        for (c0, n, vd) in ((16, 4, vA_d), (36, 8, vB_d)):
            for j in range(n):
                pi = gemm_tm(win, c0 + j, 32, lhs, hk, l)
                evac_store(pi, vd[j, :, tok], "kv")
        rmsnorm(lambda c: cqT[:, c, :], lambda c: "big0", 8, 1024, o_gq + l * 8, lambda c: ncq[:, c, :], smk)
        rmsnorm(lambda c: ckvT[:, c, :], lambda c: "big1", 4, 512, o_gkv + l * 4, lambda c: nckv[:, c, :], lambda c: "sm%d" % (4 + c // 2))
        wq = Wc[("qb", l)]
        rq = lambda kc: ncq[:, kc, :]
        for h in range(12):
            gemm_fm(wq, [2 * h], 8, rq, smk, lambda ci, pi, h=h: evac_store(pi, qCn_d[h, :, tok], "q"), l)
            pp = []
            gemm_fm(wq, [2 * h + 1], 8, rq, smk, lambda ci, pi: pp.append(pi), l, M=64, lo=0)
            gemm_fm(wq, [2 * h + 1], 8, rq, smk, lambda ci, pi: pp.append(pi), l, M=64, lo=64)
            rope_store(pp[0], pp[1], qCr_d[h, :, tok], "q")
        wkv = Wc[("kvb", l)]
        rk = lambda kc: nckv[:, kc, :]
        kvk = lambda c: "sm%d" % (4 + c // 2)
        lk = lambda kc, tb: nckv[:, kc, tb * 128:(tb + 1) * 128]
        for h in range(12):
            gemm_fm(wkv, [2 * h], 4, rk, kvk, lambda ci, pi, h=h: evac_store(pi, kCn_d[h, :, tok], "kv"), l)
            pi = gemm_tm(wkv, 2 * h + 1, 4, lk, kvk, l)
            evac_store(pi, vC_d[h, :, tok], "kv")

    rcp = rcp2[:, 0:512]
    ost = [rcp2[:, 512:768].bitcast(BF16), rcp2[:, 768:1024].bitcast(BF16)]
    ostk = ["rc1a", "rc1b"]

    def attn_finish(po, pd, nq, dst, dkey, sink_ap=None, q="pool"):
        if sink_ap is not None:
            TS("dve", rcp[:, 0:nq], psb[pd][:, 0:nq], sink_ap, None, ALU.add, None, ["ps%d" % pd, "esink"], ["rc0"])
            P.op("dve", lambda e: e.reciprocal(rcp[:, 0:nq], rcp[:, 0:nq]), ["rc0"], ["rc0"])
        else:
            P.op("dve", lambda e: e.reciprocal(rcp[:, 0:nq], psb[pd][:, 0:nq]), ["ps%d" % pd], ["rc0"])
        oi = nxt("ost", 2)
        TT("dve", ost[oi][:, 0:nq], psb[po][:, 0:nq], rcp[:, 0:nq], ALU.mult, ["ps%d" % po, "rc0"], [ostk[oi]])
        if isinstance(dst, tuple):
            CP("pool", dst[0], ost[oi][:, 0:nq], [ostk[oi]], [dst[1]])
        else:
            DMA(dst, ost[oi][:, 0:nq], [ostk[oi]], [dkey], "Sost%d" % oi, q=q, multi=True)

    assert 2 * SK <= 16384 and SK >= 2816
    kbuf = [big[:, i * SK:(i + 1) * SK] for i in range(2)]
    kbk = [RK("big", i * SK, (i + 1) * SK) for i in range(2)]
    vflat = [hflat[:, i * SK:(i + 1) * SK] for i in range(2)]
    vbuf = [vflat[i].rearrange("p (b f) -> p b f", f=128) for i in range(2)]
    vbk = [RK("hT", i * SK, (i + 1) * SK) for i in range(2)]
    krbuf = xflat[:, 0:4096].bitcast(BF16)[0:64, 0:SK]
    qbuf = [small[:, i * 1024:i * 1024 + 512] for i in range(2)]
    qrbuf = [small[0:64, 2048 + i * 1024:2048 + i * 1024 + 512] for i in range(2)]
    qbk = [["sm0", "sm2"], ["sm1", "sm3"]]

    def xr_rows(c, u):
        return slice((c * NU + u) * 128, (c * NU + u + 1) * 128)

    sel_ap = lambda c: consts[:, o_sel + c:o_sel + c + 1]

    def exchange(l):
        pin = [big[:, 0:4096], hflat[:, 0:4096]]
        pink = ["big0", "hT0"]
        pout = [big[:, 8192:12288], hflat[:, 8192:12288]]
        poutk = ["big1", "hT1"]
        n_o = 0
        grp = [list(g) for g in groups]
        order = [12] + [v for hp in range(6) for v in (hp, 6 + hp)] + [13, 14, 15, 16]
        for ui, u in enumerate(order):
            ib, ik = pin[ui % 2], pink[ui % 2]
            if u < 12:
                src = kCn_d if u < 6 else vC_d
                h0 = 2 * (u % 6)
                DMA(ib[:, 0:2048], src[h0], ["kv"], [ik], "Lpk%d" % (ui % 2))
                DMA(ib[:, 2048:4096], src[h0 + 1], ["kv"], [ik], "Lpk%d" % (ui % 2), multi=True)
            elif u == 12:
                P.op("pool", lambda e, o=ib[64:128, 0:2048]: e.memset(o, 0.0), [], [ik])
                DMA(ib[0:64, 0:2048], kCr_d[:, :], ["kv"], [ik], "Lpk%d" % (ui % 2), multi=True)
                for i4, (src, lo) in enumerate(((kA_d, 0), (kA_d, S - 128), (vA_d, 0), (vA_d, S - 128))):
                    DMA(ib[:, 2048 + i4 * 512:2048 + (i4 + 1) * 512].rearrange("p (g n) -> p g n", g=4),
                        src[:, :, lo:lo + 128].rearrange("g p n -> p g n"), ["kv"], [ik], "Lpk%d" % (ui % 2), multi=True)
            else:
                src, lo = ((kB_d, 0), (kB_d, S - 384), (vB_d, 0), (vB_d, S - 384))[u - 13]
                for g0 in range(0, 8, 4):
                    DMA(ib[:, g0 * 384:(g0 + 4) * 384].rearrange("p (g n) -> p g n", g=4),
                        src[g0:g0 + 4, :, lo:lo + 384].rearrange("g p n -> p g n"), ["kv"], [ik], "Lpk%d" % (ui % 2), multi=(g0 > 0))
            for j in range(4):
                ob, ok_ = pout[n_o % 2], poutk[n_o % 2]
                n_o += 1
                TS("dve", ob, ib, sel_ap(j), None, ALU.mult, None, [ik, "consts"], [ok_])
                DMA(XS_d[xr_rows(j, u), :], ob, [ok_], ["XS%d" % u], "Spk%d" % ((n_o - 1) % 2), q="pool", multi=True)
        for u in order:
            for j in range(4):
                rws = xr_rows(j, u)
                P.cc(lambda e, i=XS_d[rws, :], o=XR_d[rws, :]: e.collective_compute("AllReduce", ALU.add, replica_groups=grp, ins=[i], outs=[o]),
                     ["XS%d" % u], ["XR%d" % u, "ccchain"], "cc")

    def halo_assemble(l):
        cb = [big[:, k * 3072:(k + 1) * 3072] for k in range(3)]
        acc = hflat[:, 0:3072]
        halos = []
        for i4 in range(4):
            halos.append((hA_d[i4], 12, 2048 + (i4 ^ 1) * 512, 512, i4 % 2 == 0))
        for i4 in range(4):
            halos.append((hB_d[i4], 13 + (i4 ^ 1), 0, 3072, i4 % 2 == 0))
        for (dst, u, off, wd, is_prev) in halos:
            cands = [(c, c - 1) for c in (1, 2, 3)] if is_prev else [(c, c + 1) for c in (0, 1, 2)]
            for k, (c, slot) in enumerate(cands):
                DMA(cb[k][:, 0:wd], XR_d[xr_rows(slot, u), off:off + wd], ["XR%d" % u], ["big0", "big1"], "Lcb", multi=(k > 0))
            TS("dve", acc[:, 0:wd], cb[0][:, 0:wd], sel_ap(cands[0][0]), None, ALU.mult, None, ["big0", "big1", "consts"], ["hT0"])
            for k in (1, 2):
                STT("dve", acc[:, 0:wd], cb[k][:, 0:wd], sel_ap(cands[k][0]), acc[:, 0:wd], ALU.mult, ALU.add, ["big0", "big1", "consts", "hT0"], ["hT0"])
            DMA(dst, acc[:, 0:wd], ["hT0"], ["halo"], "Sacc", q="pool", multi=True)

    def phaseB(l):
        DMA(maskB, maskB_d[:, :], [], ["big2", "bg2", "bg3"], "LmaskB")
        DMA(krbuf[:, 0:S], XR_d[xr_rows(0, 12), 0:S][0:64, :], ["XR12"], ["xT0"], "Lkrb")
        for c in range(1, 4):
            DMA(krbuf[:, c * S:(c + 1) * S], XR_d[xr_rows(c, 12), 0:S][0:64, :], ["XR12"], ["xT0"], "Lkrb", multi=True)
        sc = 1.0 / math.sqrt(192.0)
        for h in range(12):
            ki = nxt("kv", 2)
            for c in range(4):
                DMA(kbuf[ki][:, c * S:(c + 1) * S], XR_d[xr_rows(c, h // 2), (h % 2) * 2048:(h % 2) * 2048 + S], ["XR%d" % (h // 2)], kbk[ki], "Lkb%d" % ki, multi=(c > 0))
                DMA(vflat[ki][:, c * S:(c + 1) * S], XR_d[xr_rows(c, 6 + h // 2), (h % 2) * 2048:(h % 2) * 2048 + S], ["XR%d" % (6 + h // 2)], vbk[ki], "Lvb%d" % ki, multi=(c > 0))
            for qt in range(NT):
                tok = slice(qt * 512, (qt + 1) * 512)
                qi = nxt("q", 2)
                DMA(qbuf[qi], qCn_d[h, :, tok], ["q"], [qbk[qi][0]], "Lqb%d" % qi)
                DMA(qrbuf[qi], qCr_d[h, :, tok], ["q"], [qbk[qi][1]], "Lqr%d" % qi)
                po = 3 + (qt % 2)
                pd = 5 + (qt % 2)

                def s_mm(kb):
                    pi = kb % 3
                    MM(psb[pi][:, :], kbuf[ki][:, kb * 128:(kb + 1) * 128], qbuf[qi], True, False, kbk[ki] + [qbk[qi][0]], ["ps%d" % pi])
                    MM(psb[pi][:, :], krbuf[:, kb * 128:(kb + 1) * 128], qrbuf[qi], False, True, ["xT0", qbk[qi][1]], ["ps%d" % pi])
                s_mm(0)
                s_mm(1)
                for kb in range(NBK):
                    if kb + 2 < NBK:
                        s_mm(kb + 2)
                    pi = kb % 3
                    gi = nxt("stg", 3)
                    ACT(stg[gi][:, 0:512], psb[pi][:, :], AF.Exp, ["ps%d" % pi], ["stg%d" % gi], scale=sc)
                    MM(psb[po][:, :], vbuf[ki][:, kb, :], stg[gi][:, 0:512], kb == 0, kb == NBK - 1, vbk[ki] + ["stg%d" % gi], ["ps%d" % po])
                    MM(psb[pd][:, :], ones[:, :], stg[gi][:, 0:512], kb == 0, kb == NBK - 1, ["ones", "stg%d" % gi], ["ps%d" % pd])
                attn_finish(po, pd, 512, cat_d[20 + h, :, tok], "cat", q="sp")
        halo_assemble(l)
        def pipeline(iters):
            if iters:
                iters[0][0]()
            for i in range(len(iters)):
                if i + 1 < len(iters):
                    iters[i + 1][0]()
                iters[i][1]()

        sc = 1.0 / math.sqrt(128.0)
        WA = (NB + 2) * 128
        itersA = []
        stA_ = {"it": 0}
        for g in range(4):
            for r in range(3):
                for qt in range(NT):
                    for qq in range(4):
                        ctx = {}

                        def s1(g=g, r=r, qt=qt, qq=qq, ctx=ctx):
                            h = g * 3 + r
                            if r == 0 and qt == 0 and qq == 0:
                                ki = nxt("kv", 2)
                                stA_["ki"] = ki
                                gs = slice(g * 128, (g + 1) * 128)
                                DMA(kbuf[ki][:, 0:128], hA_d[0][:, gs], ["halo"], kbk[ki], "Lkb%d" % ki)
                                DMA(kbuf[ki][:, 128:128 + S], kA_d[g], ["kv"], kbk[ki], "Lkb%d" % ki, multi=True)
                                DMA(kbuf[ki][:, 128 + S:WA], hA_d[1][:, gs], ["halo"], kbk[ki], "Lkb%d" % ki, multi=True)
                                DMA(vflat[ki][:, 0:128], hA_d[2][:, gs], ["halo"], vbk[ki], "Lvb%d" % ki)
                                DMA(vflat[ki][:, 128:128 + S], vA_d[g], ["kv"], vbk[ki], "Lvb%d" % ki, multi=True)
                                DMA(vflat[ki][:, 128 + S:WA], hA_d[3][:, gs], ["halo"], vbk[ki], "Lvb%d" % ki, multi=True)
                            ki = stA_["ki"]
                            if qq == 0:
                                qi = nxt("q", 2)
                                stA_["qi"] = qi
                                DMA(qbuf[qi], qA_d[h, :, qt * 512:(qt + 1) * 512], ["q"], [qbk[qi][0]], "Lqb%d" % qi)
                                stA_["po"] = 3 + (stA_["it"] % 2)
                                stA_["pd"] = 5 + (stA_["it"] % 2)
                                stA_["it"] += 1
                            qi = stA_["qi"]
                            ctx.update(ki=ki, qi=qi, po=stA_["po"], pd=stA_["pd"])
                            qb = qt * 4 + qq
                            pi = nxt("ps3", 3)
                            ctx["pi"] = pi
                            qs = qbuf[qi][:, qq * 128:(qq + 1) * 128]
                            for j in range(3):
                                wb = qb + j
                                MM(psb[pi][:, j * 128:(j + 1) * 128], kbuf[ki][:, wb * 128:(wb + 1) * 128], qs, True, True,
                                   kbk[ki] + [qbk[qi][0]], ["ps%d" % pi])

                        def s2(g=g, r=r, qt=qt, qq=qq, ctx=ctx):
                            h = g * 3 + r
                            ki, qi, po, pd, pi = ctx["ki"], ctx["qi"], ctx["po"], ctx["pd"], ctx["pi"]
                            if qt == 0 and qq == 0:
                                DMA(bias_sb[:, 0:384], biasA_d[:, h * 384:(h + 1) * 384], [], ["bias"], "Lbias")
                            qb = qt * 4 + qq
                            ti = nxt("tmp", 2)
                            STT("dve", tmpf[ti][:, 0:384], psb[pi][:, 0:384], sc, bias_sb[:, 0:384], ALU.mult, ALU.add,
                                ["ps%d" % pi, "bias"], ["tmp%d" % ti])
                            gi = nxt("stg", 3)
                            ACT(stg[gi][:, 0:384], tmpf[ti][:, 0:384], AF.Exp, ["tmp%d" % ti], ["stg%d" % gi])
                            if qb == 0:
                                TS("dve", stg[gi][:, 0:128], stg[gi][:, 0:128], consts[:, o_edge:o_edge + 1], None, ALU.mult, None,
                                   ["stg%d" % gi, "consts"], ["stg%d" % gi])
                            if qb == NB - 1:
                                TS("dve", stg[gi][:, 256:384], stg[gi][:, 256:384], consts[:, o_edge + 1:o_edge + 2], None, ALU.mult, None,
                                   ["stg%d" % gi, "consts"], ["stg%d" % gi])
                            for j in range(3):
                                wb = qb + j
                                MM(psb[po][:, qq * 128:(qq + 1) * 128], vbuf[ki][:, wb, :], stg[gi][:, j * 128:(j + 1) * 128], j == 0, j == 2,
                                   vbk[ki] + ["stg%d" % gi], ["ps%d" % po])
                            for j in range(3):
                                MM(psb[pd][:, qq * 128:(qq + 1) * 128], ones[:, :], stg[gi][:, j * 128:(j + 1) * 128], j == 0, j == 2,
                                   ["ones", "stg%d" % gi], ["ps%d" % pd])
                            if qq == 3:
                                attn_finish(po, pd, 512, cat_d[h, :, qt * 512:(qt + 1) * 512], "cat",
                                            sink_ap=consts[:, o_esink + l * 12 + h:o_esink + l * 12 + h + 1])

                        itersA.append((s1, s2))
        pipeline(itersA)
        WB = (NB + 6) * 128
        itersB = []
        stB_ = {"it": 0}
        for h in range(8):
            for qt in range(NT):
                for qq in range(4):
                    ctx = {}

                    def s1(h=h, qt=qt, qq=qq, ctx=ctx):
                        if qt == 0 and qq == 0:
                            ki = nxt("kv", 2)
                            stB_["ki"] = ki
                            hs = slice(h * 384, (h + 1) * 384)
                            DMA(kbuf[ki][:, 0:384], hB_d[0][:, hs], ["halo"], kbk[ki], "Lkb%d" % ki)
                            DMA(kbuf[ki][:, 384:384 + S], kB_d[h], ["kv"], kbk[ki], "Lkb%d" % ki, multi=True)
                            DMA(kbuf[ki][:, 384 + S:WB], hB_d[1][:, hs], ["halo"], kbk[ki], "Lkb%d" % ki, multi=True)
                            DMA(vflat[ki][:, 0:384], hB_d[2][:, hs], ["halo"], vbk[ki], "Lvb%d" % ki)
                            DMA(vflat[ki][:, 384:384 + S], vB_d[h], ["kv"], vbk[ki], "Lvb%d" % ki, multi=True)
                            DMA(vflat[ki][:, 384 + S:WB], hB_d[3][:, hs], ["halo"], vbk[ki], "Lvb%d" % ki, multi=True)
                        ki = stB_["ki"]
                        if qq == 0:
                            qi = nxt("q", 2)
                            stB_["qi"] = qi
                            DMA(qbuf[qi], qB_d[h, :, qt * 512:(qt + 1) * 512], ["q"], [qbk[qi][0]], "Lqb%d" % qi)
                            stB_["po"] = 3 + (stB_["it"] % 2)
                            stB_["pd"] = 5 + (stB_["it"] % 2)
                            stB_["it"] += 1
                        qi = stB_["qi"]
                        ctx.update(ki=ki, qi=qi, po=stB_["po"], pd=stB_["pd"])
                        m = qt * 4 + qq
                        qs = qbuf[qi][:, qq * 128:(qq + 1) * 128]
                        pa, pb_ = ((0, 1), (2, 7))[m % 2]
                        for o in range(-3, 4):
                            wb = m + o + 3
                            bank, col = (pa, (o + 3) * 128) if o <= 0 else (pb_, (o - 1) * 128)
                            MM(psb[bank][:, col:col + 128], kbuf[ki][:, wb * 128:(wb + 1) * 128], qs, True, True,
                               kbk[ki] + [qbk[qi][0]], ["ps%d" % bank])

                    def s2(h=h, qt=qt, qq=qq, ctx=ctx):
                        ki, qi, po, pd = ctx["ki"], ctx["qi"], ctx["po"], ctx["pd"]
                        if qt == 0 and qq == 0:
                            DMA(bias_sb[:, 0:896], bT_d[l * 8 + h], [], ["bias"], "Lbias")
                        m = qt * 4 + qq
                        cls = 0 if m == 0 else 1 if m == 1 else 3 if m == NB - 2 else 4 if m == NB - 1 else 2
                        pa, pb_ = ((0, 1), (2, 7))[m % 2]
                        ti = nxt("tmp", 2)
                        gi = nxt("stg", 3)
                        STT("dve", tmpf[ti][:, 0:512], psb[pa][:, 0:512], sc, bias_sb[:, 0:512], ALU.mult, ALU.add,
                            ["ps%d" % pa, "bias"], ["tmp%d" % ti])
                        STT("dve", tmpf[ti][:, 512:896], psb[pb_][:, 0:384], sc, bias_sb[:, 512:896], ALU.mult, ALU.add,
                            ["ps%d" % pb_, "bias", "tmp%d" % ti], ["tmp%d" % ti])
                        ACT(stg[gi][:, 0:896], tmpf[ti][:, 0:896], AF.Exp, ["tmp%d" % ti], ["stg%d" % gi])
                        TT("pool", stg[gi][:, 0:896], stg[gi][:, 0:896], maskB[:, cls * 896:(cls + 1) * 896], ALU.mult,
                           ["stg%d" % gi, "big2"], ["stg%d" % gi])
                        for o in range(-3, 4):
                            wb = m + o + 3
                            c = (o + 3) * 128
                            MM(psb[po][:, qq * 128:(qq + 1) * 128], vbuf[ki][:, wb, :], stg[gi][:, c:c + 128], o == -3, o == 3,
                               vbk[ki] + ["stg%d" % gi], ["ps%d" % po])
                        for o in range(-3, 4):
                            c = (o + 3) * 128
                            MM(psb[pd][:, qq * 128:(qq + 1) * 128], ones[:, :], stg[gi][:, c:c + 128], o == -3, o == 3,
                               ["ones", "stg%d" % gi], ["ps%d" % pd])
                        if qq == 3:
                            attn_finish(po, pd, 512, cat_d[12 + h, :, qt * 512:(qt + 1) * 512], "cat")

                    itersB.append((s1, s2))
        pipeline(itersB)

    qx = small[:, 0:2048].rearrange("p (h n) -> p h n", h=4)
    ox = small[:, 2048:4096].rearrange("p (h n) -> p h n", h=4)
    kmT = small[:, 4096:5120].rearrange("p (h m) -> p h m", h=4)
    vm = small[:, 5120:6144].rearrange("p (b f) -> p b f", b=2)

    def mem_kv(l):
        mT = xflat[:, 0:8192].rearrange("p (c m) -> p c m", c=32)
        mn = hflat[:, 0:8192].rearrange("p (c m) -> p c m", c=32)
        for mb in range(2):
            mi = xflat[:, 8192 + mb * 4096:8192 + (mb + 1) * 4096]
            mk_ = "xT%d" % (2 + mb)
            DMA(mi, mem_in[mb * 128:(mb + 1) * 128, :], [], [mk_], "LxT%d" % (2 + mb))
            for c4 in range(8):
                pi = nxt("ps", 8)
                for j in range(4):
                    c = c4 * 4 + j
                    P.op("pe", lambda e, o=psb[pi][:, j * 128:(j + 1) * 128], i=mi[:, c * 128:(c + 1) * 128]: e.transpose(o, i, ident[:, :]),
                         [mk_, "ident"], ["ps%d" % pi])
                CP("dve", mT[:, c4 * 4:c4 * 4 + 4, mb * 128:(mb + 1) * 128], psb[pi][:, :].rearrange("p (c j) -> p c j", c=4),
                   ["ps%d" % pi], ["xT%d" % (c4 // 4)])
        mtk = lambda c: "xT%d" % (c // 16)
        rmsnorm(lambda c: mT[:, c, :], mtk, 32, D, o_gmem + l * 32, lambda c: mn[:, c, :], lambda c: "hT0", ntok=256)

        def ev_k(ci, pi):
            CP("act", kmT[:, ci, :], psb[pi][:, 0:256], ["ps%d" % pi], ["sm4"])
        gemm_fm(Wc[("xk", l)], range(4), 32, lambda kc: mn[:, kc, :], lambda c: "hT0", ev_k, l, ntok=256)
        for j in range(4):
            pi = gemm_tm(Wc[("xv", l)], j, 32, lambda kc, tb: mn[:, kc, tb * 128:(tb + 1) * 128], lambda c: "hT0", l, nblk=2)
            CP("act", vm[:, :, j * 128:(j + 1) * 128], psb[pi][:, 0:256].rearrange("p (b f) -> p b f", b=2), ["ps%d" % pi], ["sm5"])

    act = big[:, :].rearrange("p (c n) -> p c n", c=FH)
    ak = lambda c: "big%d" % (c // 16)

    def phaseC(l, t):
        tok = slice(t * 512, (t + 1) * 512)
        load_xT(tok)
        for hf in range(2):
            DMA(hT[:, hf * 16:(hf + 1) * 16, :], cat_d[hf * 16:(hf + 1) * 16, :, tok].rearrange("c p n -> p c n"), ["cat"], ["hT%d" % hf], "LhT%d" % hf)

        def ev_res(ci, pi):
            TT("dve", xT[:, ci, :], xT[:, ci, :], psb[pi][:, :], ALU.add, [xk(ci), "ps%d" % pi], [xk(ci)])
        gemm_fm(Wc[("out", l)], range(32), 32, lambda kc: hT[:, kc, :], hk, ev_res, l)
        rmsnorm(lambda c: xT[:, c, :], xk, 32, D, o_g + (l * 3 + 1) * 32, lambda c: hT[:, c, :], hk)

        def ev_qx(ci, pi):
            CP("act", qx[:, ci, :], psb[pi][:, :], ["ps%d" % pi], ["sm%d" % (ci // 2)])
        gemm_fm(Wc[("xq", l)], range(4), 32, lambda kc: hT[:, kc, :], hk, ev_qx, l)
        sc = 1.0 / math.sqrt(128.0)
        for h in range(4):
            po, pd = 3 + (h % 2), 5 + (h % 2)
            for kb in range(2):
                pi = nxt("ps3", 3)
                MM(psb[pi][:, :], kmT[:, h, kb * 128:(kb + 1) * 128], qx[:, h, :], True, True, ["sm4", "sm%d" % (h // 2)], ["ps%d" % pi])
                gi = nxt("stg", 3)
                ACT(stg[gi][:, 0:512], psb[pi][:, :], AF.Exp, ["ps%d" % pi], ["stg%d" % gi], scale=sc)
                MM(psb[po][:, :], vm[:, kb, h * 128:(h + 1) * 128], stg[gi][:, 0:512], kb == 0, kb == 1, ["sm5", "stg%d" % gi], ["ps%d" % po])
                MM(psb[pd][:, :], ones[:, :], stg[gi][:, 0:512], kb == 0, kb == 1, ["ones", "stg%d" % gi], ["ps%d" % pd])
            attn_finish(po, pd, 512, (ox[:, h, :], "sm%d" % (2 + h // 2)), None)
        gemm_fm(Wc[("xo", l)], range(32), 4, lambda kc: ox[:, kc, :], lambda c: "sm%d" % (2 + c // 2), ev_res, l)
        rmsnorm(lambda c: xT[:, c, :], xk, 32, D, o_g + (l * 3 + 2) * 32, lambda c: hT[:, c, :], hk)
        for hh in range(2):
            for fc in range(FH):
                f = hh * FH + fc
                sg_ = load_w(Wc[("g", l)][f], 4096, l)
                pg = nxt("ps", 8)
                for kc in range(32):
                    MM(psb[pg][:, :], wsl[sg_][:, kc * 128:(kc + 1) * 128], hT[:, kc, :], kc == 0, kc == 31, ["ws%d" % sg_, hk(kc)], ["ps%d" % pg])
                su_ = load_w(Wc[("u", l)][f], 4096, l)
                pu = nxt("ps", 8)
                for kc in range(32):
                    MM(psb[pu][:, :], wsl[su_][:, kc * 128:(kc + 1) * 128], hT[:, kc, :], kc == 0, kc == 31, ["ws%d" % su_, hk(kc)], ["ps%d" % pu])
                ti = nxt("tmp", 2)
                ACT(tmpf[ti][:, 0:512], psb[pg][:, :], AF.Silu, ["ps%d" % pg], ["tmp%d" % ti])
                TT("dve", act[:, fc, :], tmpf[ti][:, 0:512], psb[pu][:, :], ALU.mult, ["tmp%d" % ti, "ps%d" % pu], [ak(fc)])
            for oc in range(32):
                pi = nxt("ps", 8)
                base = hh * FH * 128
                s1 = load_w(Wc[("d", l)][oc, :, base:base + 4096], 4096, l)
                for kc in range(32):
                    MM(psb[pi][:, :], wsl[s1][:, kc * 128:(kc + 1) * 128], act[:, kc, :], kc == 0, False, ["ws%d" % s1, ak(kc)], ["ps%d" % pi])
                s2 = load_w(Wc[("d", l)][oc, :, base + 4096:base + FH * 128], (FH - 32) * 128, l)
                for kc in range(32, FH):
                    MM(psb[pi][:, :], wsl[s2][:, (kc - 32) * 128:(kc - 31) * 128], act[:, kc, :], False, kc == FH - 1, ["ws%d" % s2, ak(kc)], ["ps%d" % pi])
                ev_res(oc, pi)
        for qd in range(4):
            DMA(xT_d[qd * 8:(qd + 1) * 8, :, tok].rearrange("c p n -> p c n"), xT[:, qd * 8:(qd + 1) * 8, :], ["xT%d" % qd], ["xTd"], "SxT%d" % qd,
                q="pool", multi=True)

    def final(t):
        tok = slice(t * 512, (t + 1) * 512)
        load_xT(tok)
        pi = nxt("ps", 8)
        for c in range(32):
            gi = nxt("stg", 3)
            ACT(stg[gi][:, 0:512], xT[:, c, :], AF.Square, [xk(c)], ["stg%d" % gi])
            MM(psb[pi][:, :], ones[:, :], stg[gi][:, 0:512], c == 0, c == 31, ["ones", "stg%d" % gi], ["ps%d" % pi])
        TS("dve", rstd[:, :], psb[pi][:, :], 1.0 / D, EPS, ALU.mult, ALU.add, ["ps%d" % pi], ["rstd"])
        P.op("act", lambda e: e.activation(rstd[:, :], rstd[:, :], AF.Sqrt), ["rstd"], ["rstd"])
        P.op("dve", lambda e: e.reciprocal(rstd[:, :], rstd[:, :]), ["rstd"], ["rstd"])
        for c in range(32):
            eng = "dve"
            STT(eng, xT[:, c, :], xT[:, c, :], consts[:, o_g + L * 96 + c:o_g + L * 96 + c + 1], rstd[:, :], ALU.mult, ALU.mult,
                [xk(c), "rstd", "consts"], [xk(c)])
        yo = hflat.bitcast(F32)
        for tb in range(4):
            half = tb % 2
            yv = yo[:, half * 4096:(half + 1) * 4096]
            for c4 in range(8):
                pi = nxt("ps", 8)
                for j in range(4):
                    c = c4 * 4 + j
                    P.op("pe", lambda e, o=psb[pi][:, j * 128:(j + 1) * 128], i=xT[:, c, tb * 128:(tb + 1) * 128]: e.transpose(o, i, ident[:, :]),
                         [xk(c), "ident"], ["ps%d" % pi])
                eng = "dve" if c4 % 2 == 0 else "act"
                CP(eng, yv[:, c4 * 512:(c4 + 1) * 512], psb[pi][:, :], ["ps%d" % pi], ["hT%d" % half])
            DMA(out_d[t * 512 + tb * 128:t * 512 + (tb + 1) * 128, :], yv, ["hT%d" % half], ["out"], "Syo%d" % half, q="pool", multi=True)

    for l in range(L):
        P.epoch += 1
        if l == 1:
            bg_flush()
        for t in range(NT):
            phaseA(l, t)
        if l == 0:
            bg_set_nb(2)
            bg_state["phase"] = "C"
        if stop == "A":
            break
        P.epoch += 1
        exchange(l)
        phaseB(l)
        if stop == "B":
            break
        P.epoch += 1
        mem_kv(l)
        for t in range(NT):
            phaseC(l, t)
    if stop is None:
        P.epoch += 1
        for t in range(NT):
            final(t)
    srcs = {"qA": qA_d, "kA": kA_d, "vA": vA_d, "qB": qB_d, "kB": kB_d, "vB": vB_d, "qCn": qCn_d, "qCr": qCr_d, "kCn": kCn_d,
            "kCr": kCr_d, "vC": vC_d, "cat": cat_d, "xT": xT_d}
    for nm, shp, dt in taps:
        DMA(tap_d[nm], srcs[nm], ["q", "kv", "cat", "xTd"], ["tap"], "Stap", q="pool", multi=True)
    P.finalize_and_emit(nc, es)
    es.close()
    return nc, P


def host_consts(S, L, inputs, rank, SEQ=8192):
    f32 = np.float32
    c = {}
    ln = []
    for l in range(L):
        for nm in ("ln_mix", "ln_xattn", "ln_ffn"):
            ln.append(np.asarray(inputs[nm][l], f32).reshape(32, 128).T)
    ln.append(np.asarray(inputs["ln_final"], f32).reshape(32, 128).T)
    c["gains"] = np.ascontiguousarray(np.concatenate(ln, axis=1))
    c["gmem"] = np.ascontiguousarray(np.concatenate([np.asarray(inputs["ln_mem"][l], f32).reshape(32, 128).T for l in range(L)], axis=1))
    c["gq"] = np.ascontiguousarray(np.concatenate([np.asarray(inputs["c_q_norm"][l], f32).reshape(8, 128).T for l in range(L)], axis=1))
    c["gkv"] = np.ascontiguousarray(np.concatenate([np.asarray(inputs["c_kv_norm"][l], f32).reshape(4, 128).T for l in range(L)], axis=1))
    c["sink"] = np.ascontiguousarray(np.broadcast_to(np.asarray(inputs["a_sink"][:L], f32).reshape(1, L * 12), (128, L * 12)))
    c["ident"] = np.eye(128, dtype=f32)
    sel = np.zeros((128, 4), f32)
    sel[:, rank] = 1.0
    c["sel"] = sel
    edge = np.ones((128, 2), f32)
    if rank == 0:
        edge[:, 0] = 0.0
    if rank == 3:
        edge[:, 1] = 0.0
    c["edge"] = edge
    inv = (1.0 / (np.float32(10000.0) ** (np.arange(0, 64, 2, dtype=f32) / np.float32(64)))).astype(f32)
    pos = np.arange(rank * S, (rank + 1) * S, dtype=f32)
    ang = (pos[:, None] * inv[None, :]).astype(f32)
    cs, sn = np.cos(ang).astype(f32).T, np.sin(ang).astype(f32).T
    c["costab"] = np.ascontiguousarray(np.concatenate([cs, cs], 0))
    c["sintab"] = np.ascontiguousarray(np.concatenate([-sn, sn], 0))
    k = np.arange(128)[:, None]
    q = np.arange(128)[None, :]
    slopes = (2.0 ** (-8.0 * np.arange(1, 13, dtype=f32) / 12)).astype(f32)
    bA = np.zeros((128, 12, 3, 128), f32)
    for j in range(3):
        rel = (j - 1) * 128 + k - q
        for h in range(12):
            bA[:, h, j, :] = np.where(np.abs(rel) <= 128, -slopes[h] * np.abs(rel).astype(f32), NEG)
    c["biasA"] = np.ascontiguousarray(bA.reshape(128, 12 * 384))
    a = np.arange(2)[:, None, None, None, None]
    kc = np.arange(64)[None, :, None, None, None]
    o = np.arange(-3, 4)[None, None, :, None, None]
    b = np.arange(2)[None, None, None, :, None]
    qc = np.arange(64)[None, None, None, None, :]
    dr = 2 * o + a - b + 0 * kc + 0 * qc
    dc = kc - qc + 0 * a + 0 * o + 0 * b
    ok = (np.abs(dr) <= 7) & (np.abs(dc) <= 15)
    dri = np.clip(dr + 7, 0, 14)
    dci = np.clip(dc + 15, 0, 30)
    rpb = np.asarray(inputs["b_rpb"], f32)
    bT = np.zeros((L * 8, 128, 896), f32)
    for l in range(L):
        for h in range(8):
            bT[l * 8 + h] = np.where(ok, rpb[l, h][dri, dci], np.float32(0.0)).reshape(128, 896)
    c["bT"] = bT
    rows = SEQ // 64
    NBl = S // 128
    mk = np.zeros((5, 128, 896), f32)
    for cls, ml in enumerate((0, 1, 2, NBl - 2, NBl - 1)):
        m = rank * NBl + ml
        qr = 2 * m + b
        kr = 2 * (m + o) + a
        rs = np.clip(qr - 4, 0, rows - 8)
        cst = np.clip(qc - 8, 0, 48)
        valid = (kr >= rs) & (kr < rs + 8) & (kc >= cst) & (kc < cst + 16) & (kr >= 0) & (kr < rows)
        mk[cls] = valid.astype(f32).reshape(128, 896)
    c["maskB"] = np.ascontiguousarray(mk.transpose(1, 0, 2).reshape(128, 5 * 896)).astype(ml_dtypes.bfloat16)
    return c


_CACHE = {}


def run(inputs, S, L, cores, taps=(), stop=None):
    groups = tuple(tuple(range(g, g + 4)) for g in range(0, len(cores), 4))
    key = (S, L, repr(taps), stop, groups)
    if key not in _CACHE:
        _CACHE[key] = build(S, L, taps, stop, groups)[0]
    nc = _CACHE[key]
    wnames = ["w_in", "c_w_q_b", "c_w_kv_b", "w_out", "x_w_q", "x_w_k", "x_w_v", "x_w_o", "w_gate", "w_up", "w_down"]
    shared = {n: np.ascontiguousarray(np.asarray(inputs[n], np.float32)[:L]) for n in wnames}
    cst = [host_consts(S, L, inputs, r) for r in range(4)]
    in_maps = []
    for (b, r) in cores:
        m = dict(shared)
        m.update(cst[r])
        m["x"] = np.ascontiguousarray(np.asarray(inputs["x"], np.float32)[b, r * S:(r + 1) * S])
        m["mem"] = np.ascontiguousarray(np.asarray(inputs["mem"], np.float32)[b])
        in_maps.append(m)
    res = run_bass_kernel_spmd(nc, in_maps, core_ids=list(range(len(cores))))
    return res.results


def kernel(**inputs):
    cores = [(b, r) for b in range(2) for r in range(4)]
    res = run(inputs, 2048, 2, cores)
    out = np.empty((2, 8192, 4096), np.float32)
    for i, (b, r) in enumerate(cores):
        out[b, r * 2048:(r + 1) * 2048] = res[i]["out"]
    return out
```

```python
import math
import numpy as np
import ml_dtypes
import concourse.bass as bass
import concourse.mybir as mybir
from concourse.bass_utils import run_bass_kernel_spmd

F32 = mybir.dt.float32
BF16 = mybir.dt.bfloat16
ALU = mybir.AluOpType
AF = mybir.ActivationFunctionType

D = 4096
DC = 32
DFF = 11008
FC = 86
FH = 43
D_IN = 7232
NEG = -30000.0
EPS = 1e-6
WSLOT = 5504


class KeyState:
    __slots__ = ("wr_op", "wr_dma", "rd_eng", "rd_dma")

    def __init__(self):
        self.wr_op = None
        self.wr_dma = {}
        self.rd_eng = {}
        self.rd_dma = {}


class Prog:
    ENGS = ("pe", "act", "dve", "pool", "sp")

    def __init__(self):
        self.ops = []
        self.keys = {}
        self.dma_cnt = {}
        self.epoch = 0

    def _ks(self, k):
        s = self.keys.get(k)
        if s is None:
            s = self.keys[k] = KeyState()
        return s

    def _add(self, eng, fn, reads, writes, dsem=None, multi=False, dinc=16):
        op = dict(eng=eng, fn=fn, deps_op=set(), deps_dma={}, dsem=dsem, sig=False, epoch=self.epoch,
                  id=len(self.ops), dinc=dinc)
        if dsem is not None:
            c = self.dma_cnt.get(dsem, 0) + dinc
            self.dma_cnt[dsem] = c
            op["dcount"] = c
        for k in reads:
            s = self._ks(k)
            if s.wr_op is not None:
                op["deps_op"].add(s.wr_op)
            for sm, c in s.wr_dma.items():
                op["deps_dma"][sm] = max(op["deps_dma"].get(sm, 0), c)
        for k in writes:
            s = self._ks(k)
            if s.wr_op is not None:
                op["deps_op"].add(s.wr_op)
            if not (multi and dsem is not None):
                for sm, c in s.wr_dma.items():
                    op["deps_dma"][sm] = max(op["deps_dma"].get(sm, 0), c)
            for e, oid in s.rd_eng.items():
                op["deps_op"].add(oid)
            for sm, c in s.rd_dma.items():
                op["deps_dma"][sm] = max(op["deps_dma"].get(sm, 0), c)
        for k in reads:
            s = self._ks(k)
            if dsem is None:
                s.rd_eng[eng] = op["id"]
            else:
                s.rd_dma[dsem] = op["dcount"]
        for k in writes:
            s = self._ks(k)
            s.rd_eng = {}
            s.rd_dma = {}
            if dsem is None:
                s.wr_op = op["id"]
                s.wr_dma = {}
            else:
                s.wr_op = None
                if not multi:
                    s.wr_dma = {}
                s.wr_dma[dsem] = op["dcount"]
        if dsem is not None:
            op["deps_dma"].pop(dsem, None) if False else None
        self.ops.append(op)
        return op

    def op(self, eng, fn, reads=(), writes=()):
        return self._add(eng, fn, reads, writes)

    def dma(self, q, out, in_, reads, writes, sem, multi=False):
        return self._add(q, lambda e, o=out, i=in_: e.dma_start(out=o, in_=i), reads, writes, dsem=sem, multi=multi)

    def cc(self, fn, reads, writes, sem):
        return self._add("pool", fn, reads, writes, dsem=sem, dinc=1)

    def finalize_and_emit(self, nc, es):
        ops = self.ops
        for op in ops:
            for pid in op["deps_op"]:
                p = ops[pid]
                if p["eng"] == op["eng"] and op["eng"] == "pe":
                    continue
                p["sig"] = True
        cnt = {}
        for op in ops:
            if op["sig"]:
                k = (op["eng"], op["epoch"])
                cnt[k] = cnt.get(k, 0) + 1
                op["sigidx"] = cnt[k]
        sems = {}

        def getsem(k):
            if k not in sems:
                sems[k] = es.enter_context(nc.semaphore("s%d" % len(sems)))
            return sems[k]

        for op in ops:
            w = {}
            for pid in op["deps_op"]:
                p = ops[pid]
                if p["eng"] == op["eng"] and op["eng"] == "pe":
                    continue
                k = ("E", p["eng"], p["epoch"])
                w[k] = max(w.get(k, 0), p["sigidx"])
            for sm, c in op["deps_dma"].items():
                k = ("D", sm)
                w[k] = max(w.get(k, 0), c)
            op["waits"] = w
        for k in list({kk for op in ops for kk in op["waits"]}):
            getsem(k)
        for op in ops:
            if op["sig"]:
                getsem(("E", op["eng"], op["epoch"]))
            if op["dsem"] is not None:
                getsem(("D", op["dsem"]))
        per = {e: [o for o in ops if o["eng"] == e] for e in self.ENGS}
        self.nsem = len(sems)

        def run(e, handle):
            waited = {}
            for op in per[e]:
                for k, v in op["waits"].items():
                    if waited.get(k, 0) < v:
                        handle.wait_ge(sems[k], v)
                        waited[k] = v
                ins = op["fn"](handle)
                if op["dsem"] is not None:
                    if op["dinc"] == 16:
                        ins.then_inc(sems[("D", op["dsem"])], 16)
                    else:
                        ins.then_inc(sems[("D", op["dsem"])])
                elif op["sig"]:
                    ins.then_inc(sems[("E", e, op["epoch"])], 1)
            for op in per[e]:
                pass
            last = {}
            for op in per[e]:
                if op["dsem"] is not None:
                    last[("D", op["dsem"])] = op["dcount"]
            for k, v in last.items():
                if waited.get(k, 0) < v:
                    handle.wait_ge(sems[k], v)

        with nc.Block() as block:
            @block.tensor
            def _(h):
                run("pe", h)

            @block.scalar
            def _(h):
                run("act", h)

            @block.vector
            def _(h):
                run("dve", h)

            @block.gpsimd
            def _(h):
                run("pool", h)

            @block.sync
            def _(h):
                run("sp", h)


def build(S, L, taps=(), stop=None, groups=((0, 1, 2, 3), (4, 5, 6, 7))):
    from contextlib import ExitStack
    nc = bass.Bass("TRN2", target_bir_lowering=False)
    NT = S // 512
    NB = S // 128
    SK = 4 * S
    NBK = SK // 128
    NU = 17
    P = Prog()
    es = ExitStack()

    def din(name, shape, dt=F32):
        return nc.dram_tensor(name, list(shape), dt, kind="ExternalInput").ap()

    def dscr(name, shape, dt=BF16):
        return nc.dram_tensor(name, list(shape), dt).ap()

    x_in = din("x", [S, D])
    mem_in = din("mem", [256, D])
    w_in_d = din("w_in", [L, D, D_IN])
    wqb_d = din("c_w_q_b", [L, 1024, 2304])
    wkvb_d = din("c_w_kv_b", [L, 512, 3072])
    wout_d = din("w_out", [L, D, D])
    xwq_d = din("x_w_q", [L, D, 512])
    xwk_d = din("x_w_k", [L, D, 512])
    xwv_d = din("x_w_v", [L, D, 512])
    xwo_d = din("x_w_o", [L, 512, D])
    wg_d = din("w_gate", [L, D, DFF])
    wu_d = din("w_up", [L, D, DFF])
    wd_d = din("w_down", [L, DFF, D])
    NG = L * 3 * 32 + 32
    gains_d = din("gains", [128, NG])
    gmem_d = din("gmem", [128, L * 32])
    gq_d = din("gq", [128, L * 8])
    gkv_d = din("gkv", [128, L * 4])
    sink_d = din("sink", [128, L * 12])
    ident_d = din("ident", [128, 128])
    sel_d = din("sel", [128, 4])
    edge_d = din("edge", [128, 2])
    cos_d = din("costab", [64, S])
    sin_d = din("sintab", [64, S])
    biasA_d = din("biasA", [128, 12 * 384])
    bT_d = din("bT", [L * 8, 128, 896])
    maskB_d = din("maskB", [128, 5 * 896], BF16)
    out_d = nc.dram_tensor("out", [S, D], F32, kind="ExternalOutput").ap()
    tap_d = {}
    for nm, shp, dt in taps:
        tap_d[nm] = nc.dram_tensor("tap_" + nm, list(shp), dt, kind="ExternalOutput").ap()

    xT_d = dscr("xT_s", [DC, 128, S], F32)
    Wc = {}
    for l in range(L):
        Wc[("in", l)] = dscr("wc_in%d" % l, [57, 128, 32 * 128])
        Wc[("qb", l)] = dscr("wc_qb%d" % l, [24, 128, 8 * 128])
        Wc[("kvb", l)] = dscr("wc_kvb%d" % l, [24, 128, 4 * 128])
        Wc[("out", l)] = dscr("wc_out%d" % l, [32, 128, 32 * 128])
        Wc[("xq", l)] = dscr("wc_xq%d" % l, [4, 128, 32 * 128])
        Wc[("xk", l)] = dscr("wc_xk%d" % l, [4, 128, 32 * 128])
        Wc[("xv", l)] = dscr("wc_xv%d" % l, [4, 128, 32 * 128])
        Wc[("xo", l)] = dscr("wc_xo%d" % l, [32, 128, 4 * 128])
        Wc[("g", l)] = dscr("wc_g%d" % l, [FC, 128, 32 * 128])
        Wc[("u", l)] = dscr("wc_u%d" % l, [FC, 128, 32 * 128])
        Wc[("d", l)] = dscr("wc_d%d" % l, [32, 128, FC * 128])
    qA_d = dscr("qA_s", [12, 128, S])
    qB_d = dscr("qB_s", [8, 128, S])
    qCn_d = dscr("qCn_s", [12, 128, S])
    qCr_d = dscr("qCr_s", [12, 64, S])
    kA_d = dscr("kA_s", [4, 128, S])
    kB_d = dscr("kB_s", [8, 128, S])
    kCn_d = dscr("kCn_s", [12, 128, S])
    kCr_d = dscr("kCr_s", [64, S])
    vA_d = dscr("vA_s", [4, 128, S])
    vB_d = dscr("vB_s", [8, 128, S])
    vC_d = dscr("vC_s", [12, 128, S])
    cat_d = dscr("cat_s", [32, 128, S])
    XS_d = dscr("XS_s", [4 * NU * 128, 4096])
    XR_d = dscr("XR_s", [4 * NU * 128, 4096])
    hA_d = dscr("hA_s", [4, 128, 512])
    hB_d = dscr("hB_s", [4, 128, 3072])

    def sb(name, shape, dt):
        return es.enter_context(nc.sbuf_tensor(name, list(shape), dt))

    def ps(name, shape):
        return es.enter_context(nc.psum_tensor(name, list(shape), F32))

    xT = sb("xT", [128, DC, 512], F32)
    hT = sb("hT", [128, DC, 512], BF16)
    big = sb("big", [128, FH * 512], BF16)
    wsl = [sb("wsl%d" % i, [128, 4096], BF16) for i in range(3)]
    consts = sb("consts", [128, NG + L * 32 + L * 8 + L * 4 + L * 12 + L * 12 + 8], F32)
    ident = sb("ident_sb", [128, 128], F32)
    ones = sb("ones", [128, 128], BF16)
    rstd = sb("rstd", [128, 512], F32)
    tmpf = [sb("tmpf%d" % i, [128, 896], F32) for i in range(2)]
    stg = [sb("stg%d" % i, [128, 896], BF16) for i in range(3)]
    rcp2 = sb("rcp2", [128, 1024], F32)
    cs_t = rcp2[0:64, 0:512]
    sn_t = rcp2[0:64, 512:1024]
    small = sb("small", [128, 6144], BF16)
    bias_sb = sb("bias_sb", [128, 896], F32)
    maskB = big[:, 16384:16384 + 5 * 896]
    psb = [ps("psb%d" % i, [128, 512]) for i in range(8)]
    XTK = ["xT0", "xT1", "xT2", "xT3"]

    def RK(buf, lo, hi):
        gran = {"xT": 4096, "hT": 8192, "big": 8192, "sm": 1024}[buf]
        return ["%s%d" % (buf, i) for i in range(lo // gran, (hi - 1) // gran + 1)]

    def xk(c):
        return "xT%d" % (c // 8)

    def hk(c):
        return "hT%d" % (c // 16)

    o_g = 0
    o_gmem = NG
    o_gq = o_gmem + L * 32
    o_gkv = o_gq + L * 8
    o_sink = o_gkv + L * 4
    o_esink = o_sink + L * 12
    o_sel = o_esink + L * 12
    o_edge = o_sel + 4

    def DMA(out, in_, reads, writes, sem, q="sp", multi=False):
        P.dma(q, out, in_, reads, writes, sem, multi)

    def MM(out, lhsT, rhs, start, stop, reads, writes):
        P.op("pe", lambda e, o=out, a=lhsT, b=rhs, s=start, t=stop: e.matmul(o, a, b, start=s, stop=t), reads, writes)

    def ACT(out, in_, func, reads, writes, scale=1.0):
        P.op("act", lambda e, o=out, i=in_, f=func, s=scale: e.activation(o, i, f, scale=s), reads, writes)

    def TS(eng, out, in0, s1, s2, op0, op1, reads, writes):
        if s2 is None:
            P.op(eng, lambda e, o=out, i=in0, a=s1, p=op0: e.tensor_scalar(o, i, a, None, p), reads, writes)
        else:
            P.op(eng, lambda e, o=out, i=in0, a=s1, b=s2, p=op0, q=op1: e.tensor_scalar(o, i, a, b, p, q), reads, writes)

    def STT(eng, out, in0, sc, in1, op0, op1, reads, writes):
        P.op(eng, lambda e, o=out, i=in0, s=sc, j=in1, p=op0, q=op1: e.scalar_tensor_tensor(o, i, s, j, p, q), reads, writes)

    def TT(eng, out, in0, in1, opx, reads, writes):
        P.op(eng, lambda e, o=out, i=in0, j=in1, p=opx: e.tensor_tensor(o, i, j, p), reads, writes)

    def CP(eng, out, in_, reads, writes):
        if eng == "act":
            P.op("act", lambda e, o=out, i=in_: e.activation(o, i, AF.Copy), reads, writes)
        else:
            P.op(eng, lambda e, o=out, i=in_: e.tensor_copy(o, i), reads, writes)

    rr = {"ps": 0, "ps3": 0, "ws": 0, "stg": 0, "tmp": 0, "ev": 0, "ost": 0, "kv": 0, "q": 0}

    def nxt(name, n):
        v = rr[name]
        rr[name] = (v + 1) % n
        return v

    DMA(consts[:, o_g:o_g + NG], gains_d[:, :], [], ["consts"], "c0")
    DMA(consts[:, o_gmem:o_gmem + L * 32], gmem_d[:, :], [], ["consts"], "c0", multi=True)
    DMA(consts[:, o_gq:o_gq + L * 8], gq_d[:, :], [], ["consts"], "c0", multi=True)
    DMA(consts[:, o_gkv:o_gkv + L * 4], gkv_d[:, :], [], ["consts"], "c0", multi=True)
    DMA(consts[:, o_sink:o_sink + L * 12], sink_d[:, :], [], ["consts"], "c0", multi=True)
    DMA(ident[:, :], ident_d[:, :], [], ["ident"], "c1")
    DMA(consts[:, o_sel:o_sel + 4], sel_d[:, :], [], ["consts"], "c0", multi=True)
    DMA(consts[:, o_edge:o_edge + 2], edge_d[:, :], [], ["consts"], "c0", multi=True)
    P.op("pool", lambda e: e.memset(ones[:, :], 1.0), [], ["ones"])
    ACT(consts[:, o_esink:o_esink + L * 12], consts[:, o_sink:o_sink + L * 12], AF.Exp, ["consts"], ["esink"])

    xflat = xT[:, :, :].rearrange("p a b -> p (a b)")
    hflat = hT[:, :, :].rearrange("p a b -> p (a b)")

    NFG = 6
    fgb = [xflat[:, i * 1024:(i + 1) * 1024] for i in range(NFG)]
    bgb = [sb("bgb%d" % i, [128, 1024], F32) for i in range(2)]

    def reg_tasks(W2d, K, N, dstWc, key, out):
        KC = K // 128
        for c in range(N // 128):
            for k0 in range(0, KC, 8):
                kg = min(8, KC - k0)
                out.append(([(W2d[k0 * 128:(k0 + kg) * 128, c * 128:(c + 1) * 128], 0, 128)], kg, dstWc[c][:, k0 * 128:(k0 + kg) * 128], key))

    def special_tasks(l, out):
        key = "W%d" % l
        w = w_in_d[l]
        for k0 in range(0, 32, 8):
            rows = slice(k0 * 128, (k0 + 8) * 128)
            out.append(([(w[rows, 7168:7232], 0, 64), (w[rows, 7200:7232], 64, 32), (w[rows, 7168:7200], 96, 32)], 8,
                        Wc[("in", l)][56][:, k0 * 128:(k0 + 8) * 128], key))
        q = wqb_d[l]
        for h in range(12):
            b0 = h * 192
            out.append(([(q[:, b0:b0 + 128], 0, 128)], 8, Wc[("qb", l)][2 * h][:, :], key))
            out.append(([(q[:, b0 + 128:b0 + 192], 0, 64), (q[:, b0 + 160:b0 + 192], 64, 32), (q[:, b0 + 128:b0 + 160], 96, 32)], 8,
                        Wc[("qb", l)][2 * h + 1][:, :], key))

    def first_tasks(l, out, key):
        reg_tasks(w_in_d[l][:, 0:7168], D, 7168, Wc[("in", l)], key, out)
        reg_tasks(wkvb_d[l], 512, 3072, Wc[("kvb", l)], key, out)

    def rest_tasks(l, out, key):
        reg_tasks(wout_d[l], D, D, Wc[("out", l)], key, out)
        reg_tasks(xwq_d[l], D, 512, Wc[("xq", l)], key, out)
        reg_tasks(xwk_d[l], D, 512, Wc[("xk", l)], key, out)
        reg_tasks(xwv_d[l], D, 512, Wc[("xv", l)], key, out)
        reg_tasks(xwo_d[l], 512, D, Wc[("xo", l)], key, out)
        reg_tasks(wg_d[l], D, DFF, Wc[("g", l)], key, out)
        reg_tasks(wu_d[l], D, DFF, Wc[("u", l)], key, out)
        reg_tasks(wd_d[l], DFF, D, Wc[("d", l)], key, out)

    def task_load(task, buf, bkey, sem):
        pieces, kg, dst, key = task
        view = buf[:, 0:kg * 128].rearrange("p (k n) -> p k n", k=kg)
        for i, (src, off, w) in enumerate(pieces):
            DMA(view[:, :, off:off + w], src.rearrange("(k p) n -> p k n", p=128), [], [bkey], sem, multi=(i > 0))

    def task_store(task, buf, bkey, sem):
        pieces, kg, dst, key = task
        DMA(dst, buf[:, 0:kg * 128], [bkey], [key], sem, q="pool", multi=True)

    fg_tasks = []
    for l in range(L):
        special_tasks(l, fg_tasks)
    first_tasks(0, fg_tasks, "W0")
    for i, tk in enumerate(fg_tasks):
        bi = i % NFG
        task_load(tk, fgb[bi], "fg%d" % bi, "Lfg%d" % bi)
        task_store(tk, fgb[bi], "fg%d" % bi, "Sfg%d" % bi)

    bg_tasks = []
    rest_tasks(0, bg_tasks, "W0b")
    for l in range(1, L):
        first_tasks(l, bg_tasks, "W%db" % l)
        rest_tasks(l, bg_tasks, "W%db" % l)
    bg_bufs = [bgb[0][:, :], bgb[1][:, :], big[:, 16384:18432].bitcast(F32), big[:, 18432:20480].bitcast(F32)]
    bg_state = {"i": 0, "nb": 4, "slot": 0, "pending": [], "phase": "A", "callsC": 0}

    def bg_emit_store():
        tk, bi = bg_state["pending"].pop(0)
        task_store(tk, bg_bufs[bi], "bg%d" % bi, "Sbg%d" % bi)

    def bg_step(n=1):
        for _ in range(n):
            i = bg_state["i"]
            nb = bg_state["nb"]
            if i < len(bg_tasks):
                bi = bg_state["slot"] % nb
                bg_state["slot"] += 1
                task_load(bg_tasks[i], bg_bufs[bi], "bg%d" % bi, "Lbg%d" % bi)
                bg_state["pending"].append((bg_tasks[i], bi))
                bg_state["i"] = i + 1
                while len(bg_state["pending"]) > nb - 1:
                    bg_emit_store()
            elif bg_state["pending"]:
                bg_emit_store()

    def bg_set_nb(nb):
        while bg_state["pending"]:
            bg_emit_store()
        bg_state["nb"] = nb
        bg_state["slot"] = 0

    def bg_flush():
        while bg_state["i"] < len(bg_tasks):
            bg_step()
        while bg_state["pending"]:
            bg_emit_store()

    def transpose_in():
        for b in range(NB):
            h2 = b % 2
            xin = xflat[:, h2 * 4096:(h2 + 1) * 4096]
            kin = ["xT%d" % h2]
            fgk = ["fg%d" % i for i in range(h2 * 4, h2 * 4 + 4)] if b < 2 else []
            DMA(xin, x_in[b * 128:(b + 1) * 128, :], [], kin + fgk, "Lxin%d" % h2)
            xo = xflat[:, 8192 + h2 * 4096:8192 + (h2 + 1) * 4096].rearrange("p (c j) -> p c j", c=32)
            ko = ["xT%d" % (2 + h2)]
            for c4 in range(8):
                pi = nxt("ps", 8)
                for j in range(4):
                    c = c4 * 4 + j
                    P.op("pe", lambda e, o=psb[pi][:, j * 128:(j + 1) * 128], i=xin[:, c * 128:(c + 1) * 128]: e.transpose(o, i, ident[:, :]),
                         kin + ["ident"], ["ps%d" % pi])
                eng = "dve" if c4 % 2 == 0 else "act"
                CP(eng, xo[:, c4 * 4:c4 * 4 + 4, :], psb[pi][:, :].rearrange("p (c j) -> p c j", c=4), ["ps%d" % pi], ko)
            DMA(xT_d[:, :, b * 128:(b + 1) * 128].rearrange("c p n -> p c n"), xo, ko, ["xTd"], "Sxo%d" % h2, q="pool", multi=True)

    transpose_in()

    def load_w(wc_ap, nel, l):
        si = nxt("ws", 3)
        wk = ["W%d" % l] if (l == 0 and bg_state["phase"] == "A") else ["W%d" % l, "W%db" % l]
        DMA(wsl[si][:, 0:nel], wc_ap, wk, ["ws%d" % si], "Lws%d_%d" % (si, l))
        if l == 0:
            if bg_state["phase"] == "A":
                bg_step(3)
            else:
                bg_state["callsC"] += 1
                rem_calls = max(1, 1224 - bg_state["callsC"])
                rem_tasks = len(bg_tasks) - bg_state["i"]
                bg_step(max(1, min(3, -(-rem_tasks // rem_calls))))
        return si

    def load_xT(tok, extra_reads=()):
        for qd in range(4):
            DMA(xT[:, qd * 8:(qd + 1) * 8, :], xT_d[qd * 8:(qd + 1) * 8, :, tok].rearrange("c p n -> p c n"),
                ["xTd"] + list(extra_reads), ["xT%d" % qd], "LxT%d" % qd)

    def rmsnorm(src_fn, src_key_fn, nchunk, dim, gain_off, dst_fn, dst_key_fn, ntok=512):
        pi = nxt("ps", 8)
        for c in range(nchunk):
            gi = nxt("stg", 3)
            ACT(stg[gi][:, 0:ntok], src_fn(c), AF.Square, [src_key_fn(c)], ["stg%d" % gi])
            MM(psb[pi][:, 0:ntok], ones[:, :], stg[gi][:, 0:ntok], c == 0, c == nchunk - 1, ["ones", "stg%d" % gi], ["ps%d" % pi])
        TS("dve", rstd[:, 0:ntok], psb[pi][:, 0:ntok], 1.0 / dim, EPS, ALU.mult, ALU.add, ["ps%d" % pi], ["rstd"])
        P.op("act", lambda e: e.activation(rstd[:, 0:ntok], rstd[:, 0:ntok], AF.Sqrt), ["rstd"], ["rstd"])
        P.op("dve", lambda e: e.reciprocal(rstd[:, 0:ntok], rstd[:, 0:ntok]), ["rstd"], ["rstd"])
        for c in range(nchunk):
            eng = "dve"
            STT(eng, dst_fn(c), src_fn(c), consts[:, gain_off + c:gain_off + c + 1], rstd[:, 0:ntok], ALU.mult, ALU.mult,
                [src_key_fn(c), "rstd", "consts"], [dst_key_fn(c)])

    def gemm_fm(wc, chunks, KC, rhs_fn, rhs_key_fn, evac, l, M=128, ntok=512, lo=0):
        for ci in chunks:
            si = load_w(wc[ci], KC * 128, l)
            pi = nxt("ps", 8)
            for kc in range(KC):
                MM(psb[pi][0:M, 0:ntok], wsl[si][:, kc * 128 + lo:kc * 128 + lo + M], rhs_fn(kc), kc == 0, kc == KC - 1,
                   ["ws%d" % si, rhs_key_fn(kc)], ["ps%d" % pi])
            evac(ci, pi)

    def gemm_tm(wc, ci, KC, lhs_fn, lhs_key_fn, l, nblk=4):
        si = load_w(wc[ci], KC * 128, l)
        pi = nxt("ps", 8)
        for tb in range(nblk):
            for kc in range(KC):
                MM(psb[pi][:, tb * 128:(tb + 1) * 128], lhs_fn(kc, tb), wsl[si][:, kc * 128:(kc + 1) * 128], kc == 0, kc == KC - 1,
                   ["ws%d" % si, lhs_key_fn(kc)], ["ps%d" % pi])
        return pi

    def evac_store(pi, dst, dkey, M=128, n=512):
        gi = nxt("stg", 3)
        eng = "act" if nxt("ev", 2) == 0 else "dve"
        CP(eng, stg[gi][0:M, 0:n], psb[pi][0:M, 0:n], ["ps%d" % pi], ["stg%d" % gi])
        DMA(dst, stg[gi][0:M, 0:n], ["stg%d" % gi], [dkey], "Sstg%d" % gi, q="pool", multi=True)

    def rope_store(pa, pb, dst, dkey):
        t0, t1 = tmpf[0], tmpf[1]
        TT("dve", t0[0:64, 0:512], psb[pa][0:64, :], cs_t, ALU.mult, ["ps%d" % pa, "cs"], ["tmp0"])
        TT("dve", t1[0:64, 0:512], psb[pb][0:64, :], sn_t, ALU.mult, ["ps%d" % pb, "cs"], ["tmp1"])
        gi = nxt("stg", 3)
        TT("dve", stg[gi][0:64, 0:512], t0[0:64, 0:512], t1[0:64, 0:512], ALU.add, ["tmp0", "tmp1"], ["stg%d" % gi])
        DMA(dst, stg[gi][0:64, 0:512], ["stg%d" % gi], [dkey], "Sstg%d" % gi, q="pool", multi=True)

    ncq = small[:, 0:4096].rearrange("p (c n) -> p c n", c=8)
    nckv = small[:, 4096:6144].rearrange("p (c n) -> p c n", c=4)
    cqT = big[:, 0:8192].bitcast(F32).rearrange("p (c n) -> p c n", c=8)
    ckvT = big[:, 8192:12288].bitcast(F32).rearrange("p (c n) -> p c n", c=4)
    smk = lambda c: "sm%d" % (c // 2)

    def phaseA(l, t):
        tok = slice(t * 512, (t + 1) * 512)
        load_xT(tok)
        DMA(cs_t, cos_d[:, tok], [], ["cs", "rc0", "rc1a", "rc1b"], "Lcs")
        DMA(sn_t, sin_d[:, tok], [], ["cs"], "Lcs", multi=True)
        rmsnorm(lambda c: xT[:, c, :], xk, 32, D, o_g + (l * 3 + 0) * 32, lambda c: hT[:, c, :], hk)
        win = Wc[("in", l)]
        rhs = lambda kc: hT[:, kc, :]

        def ev_q(dst3, base, key):
            return lambda ci, pi: evac_store(pi, dst3[ci - base, :, tok], key)

        gemm_fm(win, range(0, 12), 32, rhs, hk, ev_q(qA_d, 0, "q"), l)
        gemm_fm(win, range(12, 16), 32, rhs, hk, ev_q(kA_d, 12, "kv"), l)
        gemm_fm(win, range(20, 28), 32, rhs, hk, ev_q(qB_d, 20, "q"), l)
        gemm_fm(win, range(28, 36), 32, rhs, hk, ev_q(kB_d, 28, "kv"), l)

        def ev_c(dstT, base, key):
            def f(ci, pi):
                CP("act", dstT[:, ci - base, :], psb[pi][:, :], ["ps%d" % pi], [key])
            return f

        gemm_fm(win, range(44, 52), 32, rhs, hk, ev_c(cqT, 44, "big0"), l)
        gemm_fm(win, range(52, 56), 32, rhs, hk, ev_c(ckvT, 52, "big1"), l)
        pp = []
        gemm_fm(win, [56], 32, rhs, hk, lambda ci, pi: pp.append(pi), l, M=64, lo=0)
        gemm_fm(win, [56], 32, rhs, hk, lambda ci, pi: pp.append(pi), l, M=64, lo=64)
        rope_store(pp[0], pp[1], kCr_d[:, tok], "kv")
        lhs = lambda kc, tb: hT[:, kc, tb * 128:(tb + 1) * 128]
        for (c0, n, vd) in ((16, 4, vA_d), (36, 8, vB_d)):
            for j in range(n):
                pi = gemm_tm(win, c0 + j, 32, lhs, hk, l)
                evac_store(pi, vd[j, :, tok], "kv")
        rmsnorm(lambda c: cqT[:, c, :], lambda c: "big0", 8, 1024, o_gq + l * 8, lambda c: ncq[:, c, :], smk)
        rmsnorm(lambda c: ckvT[:, c, :], lambda c: "big1", 4, 512, o_gkv + l * 4, lambda c: nckv[:, c, :], lambda c: "sm%d" % (4 + c // 2))
        wq = Wc[("qb", l)]
        rq = lambda kc: ncq[:, kc, :]
        for h in range(12):
            gemm_fm(wq, [2 * h], 8, rq, smk, lambda ci, pi, h=h: evac_store(pi, qCn_d[h, :, tok], "q"), l)
            pp = []
            gemm_fm(wq, [2 * h + 1], 8, rq, smk, lambda ci, pi: pp.append(pi), l, M=64, lo=0)
            gemm_fm(wq, [2 * h + 1], 8, rq, smk, lambda ci, pi: pp.append(pi), l, M=64, lo=64)
            rope_store(pp[0], pp[1], qCr_d[h, :, tok], "q")
        wkv = Wc[("kvb", l)]
        rk = lambda kc: nckv[:, kc, :]
        kvk = lambda c: "sm%d" % (4 + c // 2)
        lk = lambda kc, tb: nckv[:, kc, tb * 128:(tb + 1) * 128]
        for h in range(12):
            gemm_fm(wkv, [2 * h], 4, rk, kvk, lambda ci, pi, h=h: evac_store(pi, kCn_d[h, :, tok], "kv"), l)
            pi = gemm_tm(wkv, 2 * h + 1, 4, lk, kvk, l)
            evac_store(pi, vC_d[h, :, tok], "kv")

    rcp = rcp2[:, 0:512]
    ost = [rcp2[:, 512:768].bitcast(BF16), rcp2[:, 768:1024].bitcast(BF16)]
    ostk = ["rc1a", "rc1b"]

    def attn_finish(po, pd, nq, dst, dkey, sink_ap=None, q="pool"):
        if sink_ap is not None:
            TS("dve", rcp[:, 0:nq], psb[pd][:, 0:nq], sink_ap, None, ALU.add, None, ["ps%d" % pd, "esink"], ["rc0"])
            P.op("dve", lambda e: e.reciprocal(rcp[:, 0:nq], rcp[:, 0:nq]), ["rc0"], ["rc0"])
        else:
            P.op("dve", lambda e: e.reciprocal(rcp[:, 0:nq], psb[pd][:, 0:nq]), ["ps%d" % pd], ["rc0"])
        oi = nxt("ost", 2)
        TT("dve", ost[oi][:, 0:nq], psb[po][:, 0:nq], rcp[:, 0:nq], ALU.mult, ["ps%d" % po, "rc0"], [ostk[oi]])
        if isinstance(dst, tuple):
            CP("pool", dst[0], ost[oi][:, 0:nq], [ostk[oi]], [dst[1]])
        else:
            DMA(dst, ost[oi][:, 0:nq], [ostk[oi]], [dkey], "Sost%d" % oi, q=q, multi=True)

    assert 2 * SK <= 16384 and SK >= 2816
    kbuf = [big[:, i * SK:(i + 1) * SK] for i in range(2)]
    kbk = [RK("big", i * SK, (i + 1) * SK) for i in range(2)]
    vflat = [hflat[:, i * SK:(i + 1) * SK] for i in range(2)]
    vbuf = [vflat[i].rearrange("p (b f) -> p b f", f=128) for i in range(2)]
    vbk = [RK("hT", i * SK, (i + 1) * SK) for i in range(2)]
    krbuf = xflat[:, 0:4096].bitcast(BF16)[0:64, 0:SK]
    qbuf = [small[:, i * 1024:i * 1024 + 512] for i in range(2)]
    qrbuf = [small[0:64, 2048 + i * 1024:2048 + i * 1024 + 512] for i in range(2)]
    qbk = [["sm0", "sm2"], ["sm1", "sm3"]]

    def xr_rows(c, u):
        return slice((c * NU + u) * 128, (c * NU + u + 1) * 128)

    sel_ap = lambda c: consts[:, o_sel + c:o_sel + c + 1]

    def exchange(l):
        pin = [big[:, 0:4096], hflat[:, 0:4096]]
        pink = ["big0", "hT0"]
        pout = [big[:, 8192:12288], hflat[:, 8192:12288]]
        poutk = ["big1", "hT1"]
        n_o = 0
        grp = [list(g) for g in groups]
        order = [12] + [v for hp in range(6) for v in (hp, 6 + hp)] + [13, 14, 15, 16]
        for ui, u in enumerate(order):
            ib, ik = pin[ui % 2], pink[ui % 2]
            if u < 12:
                src = kCn_d if u < 6 else vC_d
                h0 = 2 * (u % 6)
                DMA(ib[:, 0:2048], src[h0], ["kv"], [ik], "Lpk%d" % (ui % 2))
                DMA(ib[:, 2048:4096], src[h0 + 1], ["kv"], [ik], "Lpk%d" % (ui % 2), multi=True)
            elif u == 12:
                P.op("pool", lambda e, o=ib[64:128, 0:2048]: e.memset(o, 0.0), [], [ik])
                DMA(ib[0:64, 0:2048], kCr_d[:, :], ["kv"], [ik], "Lpk%d" % (ui % 2), multi=True)
                for i4, (src, lo) in enumerate(((kA_d, 0), (kA_d, S - 128), (vA_d, 0), (vA_d, S - 128))):
                    DMA(ib[:, 2048 + i4 * 512:2048 + (i4 + 1) * 512].rearrange("p (g n) -> p g n", g=4),
                        src[:, :, lo:lo + 128].rearrange("g p n -> p g n"), ["kv"], [ik], "Lpk%d" % (ui % 2), multi=True)
            else:
                src, lo = ((kB_d, 0), (kB_d, S - 384), (vB_d, 0), (vB_d, S - 384))[u - 13]
                for g0 in range(0, 8, 4):
                    DMA(ib[:, g0 * 384:(g0 + 4) * 384].rearrange("p (g n) -> p g n", g=4),
                        src[g0:g0 + 4, :, lo:lo + 384].rearrange("g p n -> p g n"), ["kv"], [ik], "Lpk%d" % (ui % 2), multi=(g0 > 0))
            for j in range(4):
                ob, ok_ = pout[n_o % 2], poutk[n_o % 2]
                n_o += 1
                TS("dve", ob, ib, sel_ap(j), None, ALU.mult, None, [ik, "consts"], [ok_])
                DMA(XS_d[xr_rows(j, u), :], ob, [ok_], ["XS%d" % u], "Spk%d" % ((n_o - 1) % 2), q="pool", multi=True)
        for u in order:
            for j in range(4):
                rws = xr_rows(j, u)
                P.cc(lambda e, i=XS_d[rws, :], o=XR_d[rws, :]: e.collective_compute("AllReduce", ALU.add, replica_groups=grp, ins=[i], outs=[o]),
                     ["XS%d" % u], ["XR%d" % u, "ccchain"], "cc")

    def halo_assemble(l):
        cb = [big[:, k * 3072:(k + 1) * 3072] for k in range(3)]
        acc = hflat[:, 0:3072]
        halos = []
        for i4 in range(4):
            halos.append((hA_d[i4], 12, 2048 + (i4 ^ 1) * 512, 512, i4 % 2 == 0))
        for i4 in range(4):
            halos.append((hB_d[i4], 13 + (i4 ^ 1), 0, 3072, i4 % 2 == 0))
        for (dst, u, off, wd, is_prev) in halos:
            cands = [(c, c - 1) for c in (1, 2, 3)] if is_prev else [(c, c + 1) for c in (0, 1, 2)]
            for k, (c, slot) in enumerate(cands):
                DMA(cb[k][:, 0:wd], XR_d[xr_rows(slot, u), off:off + wd], ["XR%d" % u], ["big0", "big1"], "Lcb", multi=(k > 0))
            TS("dve", acc[:, 0:wd], cb[0][:, 0:wd], sel_ap(cands[0][0]), None, ALU.mult, None, ["big0", "big1", "consts"], ["hT0"])
            for k in (1, 2):
                STT("dve", acc[:, 0:wd], cb[k][:, 0:wd], sel_ap(cands[k][0]), acc[:, 0:wd], ALU.mult, ALU.add, ["big0", "big1", "consts", "hT0"], ["hT0"])
            DMA(dst, acc[:, 0:wd], ["hT0"], ["halo"], "Sacc", q="pool", multi=True)

    def phaseB(l):
        DMA(maskB, maskB_d[:, :], [], ["big2", "bg2", "bg3"], "LmaskB")
        DMA(krbuf[:, 0:S], XR_d[xr_rows(0, 12), 0:S][0:64, :], ["XR12"], ["xT0"], "Lkrb")
        for c in range(1, 4):
            DMA(krbuf[:, c * S:(c + 1) * S], XR_d[xr_rows(c, 12), 0:S][0:64, :], ["XR12"], ["xT0"], "Lkrb", multi=True)
        sc = 1.0 / math.sqrt(192.0)
        for h in range(12):
            ki = nxt("kv", 2)
            for c in range(4):
                DMA(kbuf[ki][:, c * S:(c + 1) * S], XR_d[xr_rows(c, h // 2), (h % 2) * 2048:(h % 2) * 2048 + S], ["XR%d" % (h // 2)], kbk[ki], "Lkb%d" % ki, multi=(c > 0))
                DMA(vflat[ki][:, c * S:(c + 1) * S], XR_d[xr_rows(c, 6 + h // 2), (h % 2) * 2048:(h % 2) * 2048 + S], ["XR%d" % (6 + h // 2)], vbk[ki], "Lvb%d" % ki, multi=(c > 0))
            for qt in range(NT):
                tok = slice(qt * 512, (qt + 1) * 512)
                qi = nxt("q", 2)
                DMA(qbuf[qi], qCn_d[h, :, tok], ["q"], [qbk[qi][0]], "Lqb%d" % qi)
                DMA(qrbuf[qi], qCr_d[h, :, tok], ["q"], [qbk[qi][1]], "Lqr%d" % qi)
                po = 3 + (qt % 2)
                pd = 5 + (qt % 2)

                SB = (0, 1, 2, 7)

                def s_mm(kb):
                    pi = SB[kb % 4]
                    MM(psb[pi][:, :], kbuf[ki][:, kb * 128:(kb + 1) * 128], qbuf[qi], True, False, kbk[ki] + [qbk[qi][0]], ["ps%d" % pi])
                    MM(psb[pi][:, :], krbuf[:, kb * 128:(kb + 1) * 128], qrbuf[qi], False, True, ["xT0", qbk[qi][1]], ["ps%d" % pi])
                s_mm(0)
                s_mm(1)
                s_mm(2)
                for kb in range(NBK):
                    if kb + 3 < NBK:
                        s_mm(kb + 3)
                    pi = SB[kb % 4]
                    gi = nxt("stg", 3)
                    ACT(stg[gi][:, 0:512], psb[pi][:, :], AF.Exp, ["ps%d" % pi], ["stg%d" % gi], scale=sc)
                    MM(psb[po][:, :], vbuf[ki][:, kb, :], stg[gi][:, 0:512], kb == 0, kb == NBK - 1, vbk[ki] + ["stg%d" % gi], ["ps%d" % po])
                    MM(psb[pd][:, :], ones[:, :], stg[gi][:, 0:512], kb == 0, kb == NBK - 1, ["ones", "stg%d" % gi], ["ps%d" % pd])
                attn_finish(po, pd, 512, cat_d[20 + h, :, tok], "cat", q="sp")
        halo_assemble(l)
        def pipeline(iters):
            if iters:
                iters[0][0]()
            for i in range(len(iters)):
                if i + 1 < len(iters):
                    iters[i + 1][0]()
                iters[i][1]()

        sc = 1.0 / math.sqrt(128.0)
        WA = (NB + 2) * 128
        itersA = []
        stA_ = {"it": 0}
        for g in range(4):
            for r in range(3):
                for qt in range(NT):
                    for qq in range(4):
                        ctx = {}

                        def s1(g=g, r=r, qt=qt, qq=qq, ctx=ctx):
                            h = g * 3 + r
                            if r == 0 and qt == 0 and qq == 0:
                                ki = nxt("kv", 2)
                                stA_["ki"] = ki
                                gs = slice(g * 128, (g + 1) * 128)
                                DMA(kbuf[ki][:, 0:128], hA_d[0][:, gs], ["halo"], kbk[ki], "Lkb%d" % ki)
                                DMA(kbuf[ki][:, 128:128 + S], kA_d[g], ["kv"], kbk[ki], "Lkb%d" % ki, multi=True)
                                DMA(kbuf[ki][:, 128 + S:WA], hA_d[1][:, gs], ["halo"], kbk[ki], "Lkb%d" % ki, multi=True)
                                DMA(vflat[ki][:, 0:128], hA_d[2][:, gs], ["halo"], vbk[ki], "Lvb%d" % ki)
                                DMA(vflat[ki][:, 128:128 + S], vA_d[g], ["kv"], vbk[ki], "Lvb%d" % ki, multi=True)
                                DMA(vflat[ki][:, 128 + S:WA], hA_d[3][:, gs], ["halo"], vbk[ki], "Lvb%d" % ki, multi=True)
                            ki = stA_["ki"]
                            if qq == 0:
                                qi = nxt("q", 2)
                                stA_["qi"] = qi
                                DMA(qbuf[qi], qA_d[h, :, qt * 512:(qt + 1) * 512], ["q"], [qbk[qi][0]], "Lqb%d" % qi)
                                stA_["po"] = 3 + (stA_["it"] % 2)
                                stA_["pd"] = 5 + (stA_["it"] % 2)
                                stA_["it"] += 1
                            qi = stA_["qi"]
                            ctx.update(ki=ki, qi=qi, po=stA_["po"], pd=stA_["pd"])
                            qb = qt * 4 + qq
                            pi = nxt("ps3", 3)
                            ctx["pi"] = pi
                            qs = qbuf[qi][:, qq * 128:(qq + 1) * 128]
                            for j in range(3):
                                wb = qb + j
                                MM(psb[pi][:, j * 128:(j + 1) * 128], kbuf[ki][:, wb * 128:(wb + 1) * 128], qs, True, True,
                                   kbk[ki] + [qbk[qi][0]], ["ps%d" % pi])

                        def s2(g=g, r=r, qt=qt, qq=qq, ctx=ctx):
                            h = g * 3 + r
                            ki, qi, po, pd, pi = ctx["ki"], ctx["qi"], ctx["po"], ctx["pd"], ctx["pi"]
                            if qt == 0 and qq == 0:
                                DMA(bias_sb[:, 0:384], biasA_d[:, h * 384:(h + 1) * 384], [], ["bias"], "Lbias")
                            qb = qt * 4 + qq
                            ti = nxt("tmp", 2)
                            STT("dve", tmpf[ti][:, 0:384], psb[pi][:, 0:384], sc, bias_sb[:, 0:384], ALU.mult, ALU.add,
                                ["ps%d" % pi, "bias"], ["tmp%d" % ti])
                            gi = nxt("stg", 3)
                            ACT(stg[gi][:, 0:384], tmpf[ti][:, 0:384], AF.Exp, ["tmp%d" % ti], ["stg%d" % gi])
                            if qb == 0:
                                TS("dve", stg[gi][:, 0:128], stg[gi][:, 0:128], consts[:, o_edge:o_edge + 1], None, ALU.mult, None,
                                   ["stg%d" % gi, "consts"], ["stg%d" % gi])
                            if qb == NB - 1:
                                TS("dve", stg[gi][:, 256:384], stg[gi][:, 256:384], consts[:, o_edge + 1:o_edge + 2], None, ALU.mult, None,
                                   ["stg%d" % gi, "consts"], ["stg%d" % gi])
                            for j in range(3):
                                wb = qb + j
                                MM(psb[po][:, qq * 128:(qq + 1) * 128], vbuf[ki][:, wb, :], stg[gi][:, j * 128:(j + 1) * 128], j == 0, j == 2,
                                   vbk[ki] + ["stg%d" % gi], ["ps%d" % po])
                            for j in range(3):
                                MM(psb[pd][:, qq * 128:(qq + 1) * 128], ones[:, :], stg[gi][:, j * 128:(j + 1) * 128], j == 0, j == 2,
                                   ["ones", "stg%d" % gi], ["ps%d" % pd])
                            if qq == 3:
                                attn_finish(po, pd, 512, cat_d[h, :, qt * 512:(qt + 1) * 512], "cat",
                                            sink_ap=consts[:, o_esink + l * 12 + h:o_esink + l * 12 + h + 1])

                        itersA.append((s1, s2))
        pipeline(itersA)
        WB = (NB + 6) * 128
        itersB = []
        stB_ = {"it": 0}
        for h in range(8):
            for qt in range(NT):
                for qq in range(4):
                    ctx = {}

                    def s1(h=h, qt=qt, qq=qq, ctx=ctx):
                        if qt == 0 and qq == 0:
                            ki = nxt("kv", 2)
                            stB_["ki"] = ki
                            hs = slice(h * 384, (h + 1) * 384)
                            DMA(kbuf[ki][:, 0:384], hB_d[0][:, hs], ["halo"], kbk[ki], "Lkb%d" % ki)
                            DMA(kbuf[ki][:, 384:384 + S], kB_d[h], ["kv"], kbk[ki], "Lkb%d" % ki, multi=True)
                            DMA(kbuf[ki][:, 384 + S:WB], hB_d[1][:, hs], ["halo"], kbk[ki], "Lkb%d" % ki, multi=True)
                            DMA(vflat[ki][:, 0:384], hB_d[2][:, hs], ["halo"], vbk[ki], "Lvb%d" % ki)
                            DMA(vflat[ki][:, 384:384 + S], vB_d[h], ["kv"], vbk[ki], "Lvb%d" % ki, multi=True)
                            DMA(vflat[ki][:, 384 + S:WB], hB_d[3][:, hs], ["halo"], vbk[ki], "Lvb%d" % ki, multi=True)
                        ki = stB_["ki"]
                        if qq == 0:
                            qi = nxt("q", 2)
                            stB_["qi"] = qi
                            DMA(qbuf[qi], qB_d[h, :, qt * 512:(qt + 1) * 512], ["q"], [qbk[qi][0]], "Lqb%d" % qi)
                            stB_["po"] = 3 + (stB_["it"] % 2)
                            stB_["pd"] = 5 + (stB_["it"] % 2)
                            stB_["it"] += 1
                        qi = stB_["qi"]
                        ctx.update(ki=ki, qi=qi, po=stB_["po"], pd=stB_["pd"])
                        m = qt * 4 + qq
                        qs = qbuf[qi][:, qq * 128:(qq + 1) * 128]
                        pa, pb_ = ((0, 1), (2, 7))[m % 2]
                        for o in range(-3, 4):
                            wb = m + o + 3
                            bank, col = (pa, (o + 3) * 128) if o <= 0 else (pb_, (o - 1) * 128)
                            MM(psb[bank][:, col:col + 128], kbuf[ki][:, wb * 128:(wb + 1) * 128], qs, True, True,
                               kbk[ki] + [qbk[qi][0]], ["ps%d" % bank])

                    def s2(h=h, qt=qt, qq=qq, ctx=ctx):
                        ki, qi, po, pd = ctx["ki"], ctx["qi"], ctx["po"], ctx["pd"]
                        if qt == 0 and qq == 0:
                            DMA(bias_sb[:, 0:896], bT_d[l * 8 + h], [], ["bias"], "Lbias")
                        m = qt * 4 + qq
                        cls = 0 if m == 0 else 1 if m == 1 else 3 if m == NB - 2 else 4 if m == NB - 1 else 2
                        pa, pb_ = ((0, 1), (2, 7))[m % 2]
                        ti = nxt("tmp", 2)
                        gi = nxt("stg", 3)
                        STT("dve", tmpf[ti][:, 0:512], psb[pa][:, 0:512], sc, bias_sb[:, 0:512], ALU.mult, ALU.add,
                            ["ps%d" % pa, "bias"], ["tmp%d" % ti])
                        STT("dve", tmpf[ti][:, 512:896], psb[pb_][:, 0:384], sc, bias_sb[:, 512:896], ALU.mult, ALU.add,
                            ["ps%d" % pb_, "bias", "tmp%d" % ti], ["tmp%d" % ti])
                        ACT(stg[gi][:, 0:896], tmpf[ti][:, 0:896], AF.Exp, ["tmp%d" % ti], ["stg%d" % gi])
                        TT("pool", stg[gi][:, 0:896], stg[gi][:, 0:896], maskB[:, cls * 896:(cls + 1) * 896], ALU.mult,
                           ["stg%d" % gi, "big2"], ["stg%d" % gi])
                        for o in range(-3, 4):
                            wb = m + o + 3
                            c = (o + 3) * 128
                            MM(psb[po][:, qq * 128:(qq + 1) * 128], vbuf[ki][:, wb, :], stg[gi][:, c:c + 128], o == -3, o == 3,
                               vbk[ki] + ["stg%d" % gi], ["ps%d" % po])
                        for o in range(-3, 4):
                            c = (o + 3) * 128
                            MM(psb[pd][:, qq * 128:(qq + 1) * 128], ones[:, :], stg[gi][:, c:c + 128], o == -3, o == 3,
                               ["ones", "stg%d" % gi], ["ps%d" % pd])
                        if qq == 3:
                            attn_finish(po, pd, 512, cat_d[12 + h, :, qt * 512:(qt + 1) * 512], "cat")

                    itersB.append((s1, s2))
        pipeline(itersB)

    qx = small[:, 0:2048].rearrange("p (h n) -> p h n", h=4)
    ox = small[:, 2048:4096].rearrange("p (h n) -> p h n", h=4)
    kmT = small[:, 4096:5120].rearrange("p (h m) -> p h m", h=4)
    vm = small[:, 5120:6144].rearrange("p (b f) -> p b f", b=2)

    def mem_kv(l):
        mT = xflat[:, 0:8192].rearrange("p (c m) -> p c m", c=32)
        mn = hflat[:, 0:8192].rearrange("p (c m) -> p c m", c=32)
        for mb in range(2):
            mi = xflat[:, 8192 + mb * 4096:8192 + (mb + 1) * 4096]
            mk_ = "xT%d" % (2 + mb)
            DMA(mi, mem_in[mb * 128:(mb + 1) * 128, :], [], [mk_], "LxT%d" % (2 + mb))
            for c4 in range(8):
                pi = nxt("ps", 8)
                for j in range(4):
                    c = c4 * 4 + j
                    P.op("pe", lambda e, o=psb[pi][:, j * 128:(j + 1) * 128], i=mi[:, c * 128:(c + 1) * 128]: e.transpose(o, i, ident[:, :]),
                         [mk_, "ident"], ["ps%d" % pi])
                CP("dve", mT[:, c4 * 4:c4 * 4 + 4, mb * 128:(mb + 1) * 128], psb[pi][:, :].rearrange("p (c j) -> p c j", c=4),
                   ["ps%d" % pi], ["xT%d" % (c4 // 4)])
        mtk = lambda c: "xT%d" % (c // 16)
        rmsnorm(lambda c: mT[:, c, :], mtk, 32, D, o_gmem + l * 32, lambda c: mn[:, c, :], lambda c: "hT0", ntok=256)

        def ev_k(ci, pi):
            CP("act", kmT[:, ci, :], psb[pi][:, 0:256], ["ps%d" % pi], ["sm4"])
        gemm_fm(Wc[("xk", l)], range(4), 32, lambda kc: mn[:, kc, :], lambda c: "hT0", ev_k, l, ntok=256)
        for j in range(4):
            pi = gemm_tm(Wc[("xv", l)], j, 32, lambda kc, tb: mn[:, kc, tb * 128:(tb + 1) * 128], lambda c: "hT0", l, nblk=2)
            CP("act", vm[:, :, j * 128:(j + 1) * 128], psb[pi][:, 0:256].rearrange("p (b f) -> p b f", b=2), ["ps%d" % pi], ["sm5"])

    act = big[:, :].rearrange("p (c n) -> p c n", c=FH)
    ak = lambda c: "big%d" % (c // 16)

    def phaseC(l, t):
        tok = slice(t * 512, (t + 1) * 512)
        load_xT(tok)
        for hf in range(2):
            DMA(hT[:, hf * 16:(hf + 1) * 16, :], cat_d[hf * 16:(hf + 1) * 16, :, tok].rearrange("c p n -> p c n"), ["cat"], ["hT%d" % hf], "LhT%d" % hf)

        def ev_res(ci, pi):
            TT("dve", xT[:, ci, :], xT[:, ci, :], psb[pi][:, :], ALU.add, [xk(ci), "ps%d" % pi], [xk(ci)])
        gemm_fm(Wc[("out", l)], range(32), 32, lambda kc: hT[:, kc, :], hk, ev_res, l)
        rmsnorm(lambda c: xT[:, c, :], xk, 32, D, o_g + (l * 3 + 1) * 32, lambda c: hT[:, c, :], hk)

        def ev_qx(ci, pi):
            CP("act", qx[:, ci, :], psb[pi][:, :], ["ps%d" % pi], ["sm%d" % (ci // 2)])
        gemm_fm(Wc[("xq", l)], range(4), 32, lambda kc: hT[:, kc, :], hk, ev_qx, l)
        sc = 1.0 / math.sqrt(128.0)
        for h in range(4):
            po, pd = 3 + (h % 2), 5 + (h % 2)
            for kb in range(2):
                pi = nxt("ps3", 3)
                MM(psb[pi][:, :], kmT[:, h, kb * 128:(kb + 1) * 128], qx[:, h, :], True, True, ["sm4", "sm%d" % (h // 2)], ["ps%d" % pi])
                gi = nxt("stg", 3)
                ACT(stg[gi][:, 0:512], psb[pi][:, :], AF.Exp, ["ps%d" % pi], ["stg%d" % gi], scale=sc)
                MM(psb[po][:, :], vm[:, kb, h * 128:(h + 1) * 128], stg[gi][:, 0:512], kb == 0, kb == 1, ["sm5", "stg%d" % gi], ["ps%d" % po])
                MM(psb[pd][:, :], ones[:, :], stg[gi][:, 0:512], kb == 0, kb == 1, ["ones", "stg%d" % gi], ["ps%d" % pd])
            attn_finish(po, pd, 512, (ox[:, h, :], "sm%d" % (2 + h // 2)), None)
        gemm_fm(Wc[("xo", l)], range(32), 4, lambda kc: ox[:, kc, :], lambda c: "sm%d" % (2 + c // 2), ev_res, l)
        rmsnorm(lambda c: xT[:, c, :], xk, 32, D, o_g + (l * 3 + 2) * 32, lambda c: hT[:, c, :], hk)
        for hh in range(2):
            for fc in range(FH):
                f = hh * FH + fc
                sg_ = load_w(Wc[("g", l)][f], 4096, l)
                pg = nxt("ps", 8)
                for kc in range(32):
                    MM(psb[pg][:, :], wsl[sg_][:, kc * 128:(kc + 1) * 128], hT[:, kc, :], kc == 0, kc == 31, ["ws%d" % sg_, hk(kc)], ["ps%d" % pg])
                su_ = load_w(Wc[("u", l)][f], 4096, l)
                pu = nxt("ps", 8)
                for kc in range(32):
                    MM(psb[pu][:, :], wsl[su_][:, kc * 128:(kc + 1) * 128], hT[:, kc, :], kc == 0, kc == 31, ["ws%d" % su_, hk(kc)], ["ps%d" % pu])
                ti = nxt("tmp", 2)
                ACT(tmpf[ti][:, 0:512], psb[pg][:, :], AF.Silu, ["ps%d" % pg], ["tmp%d" % ti])
                TT("dve", act[:, fc, :], tmpf[ti][:, 0:512], psb[pu][:, :], ALU.mult, ["tmp%d" % ti, "ps%d" % pu], [ak(fc)])
            for oc in range(32):
                pi = nxt("ps", 8)
                base = hh * FH * 128
                s1 = load_w(Wc[("d", l)][oc, :, base:base + 4096], 4096, l)
                for kc in range(32):
                    MM(psb[pi][:, :], wsl[s1][:, kc * 128:(kc + 1) * 128], act[:, kc, :], kc == 0, False, ["ws%d" % s1, ak(kc)], ["ps%d" % pi])
                s2 = load_w(Wc[("d", l)][oc, :, base + 4096:base + FH * 128], (FH - 32) * 128, l)
                for kc in range(32, FH):
                    MM(psb[pi][:, :], wsl[s2][:, (kc - 32) * 128:(kc - 31) * 128], act[:, kc, :], False, kc == FH - 1, ["ws%d" % s2, ak(kc)], ["ps%d" % pi])
                ev_res(oc, pi)
        for qd in range(4):
            DMA(xT_d[qd * 8:(qd + 1) * 8, :, tok].rearrange("c p n -> p c n"), xT[:, qd * 8:(qd + 1) * 8, :], ["xT%d" % qd], ["xTd"], "SxT%d" % qd,
                q="pool", multi=True)

    def final(t):
        tok = slice(t * 512, (t + 1) * 512)
        load_xT(tok)
        pi = nxt("ps", 8)
        for c in range(32):
            gi = nxt("stg", 3)
            ACT(stg[gi][:, 0:512], xT[:, c, :], AF.Square, [xk(c)], ["stg%d" % gi])
            MM(psb[pi][:, :], ones[:, :], stg[gi][:, 0:512], c == 0, c == 31, ["ones", "stg%d" % gi], ["ps%d" % pi])
        TS("dve", rstd[:, :], psb[pi][:, :], 1.0 / D, EPS, ALU.mult, ALU.add, ["ps%d" % pi], ["rstd"])
        P.op("act", lambda e: e.activation(rstd[:, :], rstd[:, :], AF.Sqrt), ["rstd"], ["rstd"])
        P.op("dve", lambda e: e.reciprocal(rstd[:, :], rstd[:, :]), ["rstd"], ["rstd"])
        for c in range(32):
            eng = "dve"
            STT(eng, xT[:, c, :], xT[:, c, :], consts[:, o_g + L * 96 + c:o_g + L * 96 + c + 1], rstd[:, :], ALU.mult, ALU.mult,
                [xk(c), "rstd", "consts"], [xk(c)])
        yo = hflat.bitcast(F32)
        for tb in range(4):
            half = tb % 2
            yv = yo[:, half * 4096:(half + 1) * 4096]
            for c4 in range(8):
                pi = nxt("ps", 8)
                for j in range(4):
                    c = c4 * 4 + j
                    P.op("pe", lambda e, o=psb[pi][:, j * 128:(j + 1) * 128], i=xT[:, c, tb * 128:(tb + 1) * 128]: e.transpose(o, i, ident[:, :]),
                         [xk(c), "ident"], ["ps%d" % pi])
                eng = "dve" if c4 % 2 == 0 else "act"
                CP(eng, yv[:, c4 * 512:(c4 + 1) * 512], psb[pi][:, :], ["ps%d" % pi], ["hT%d" % half])
            DMA(out_d[t * 512 + tb * 128:t * 512 + (tb + 1) * 128, :], yv, ["hT%d" % half], ["out"], "Syo%d" % half, q="pool", multi=True)

    for l in range(L):
        P.epoch += 1
        if l == 1:
            bg_flush()
        for t in range(NT):
            phaseA(l, t)
        if l == 0:
            bg_set_nb(2)
            bg_state["phase"] = "C"
        if stop == "A":
            break
        P.epoch += 1
        exchange(l)
        phaseB(l)
        if stop == "B":
            break
        P.epoch += 1
        mem_kv(l)
        for t in range(NT):
            phaseC(l, t)
    if stop is None:
        P.epoch += 1
        for t in range(NT):
            final(t)
    srcs = {"qA": qA_d, "kA": kA_d, "vA": vA_d, "qB": qB_d, "kB": kB_d, "vB": vB_d, "qCn": qCn_d, "qCr": qCr_d, "kCn": kCn_d,
            "kCr": kCr_d, "vC": vC_d, "cat": cat_d, "xT": xT_d}
    for nm, shp, dt in taps:
        DMA(tap_d[nm], srcs[nm], ["q", "kv", "cat", "xTd"], ["tap"], "Stap", q="pool", multi=True)
    P.finalize_and_emit(nc, es)
    es.close()
    return nc, P


def host_consts(S, L, inputs, rank, SEQ=8192):
    f32 = np.float32
    c = {}
    ln = []
    for l in range(L):
        for nm in ("ln_mix", "ln_xattn", "ln_ffn"):
            ln.append(np.asarray(inputs[nm][l], f32).reshape(32, 128).T)
    ln.append(np.asarray(inputs["ln_final"], f32).reshape(32, 128).T)
    c["gains"] = np.ascontiguousarray(np.concatenate(ln, axis=1))
    c["gmem"] = np.ascontiguousarray(np.concatenate([np.asarray(inputs["ln_mem"][l], f32).reshape(32, 128).T for l in range(L)], axis=1))
    c["gq"] = np.ascontiguousarray(np.concatenate([np.asarray(inputs["c_q_norm"][l], f32).reshape(8, 128).T for l in range(L)], axis=1))
    c["gkv"] = np.ascontiguousarray(np.concatenate([np.asarray(inputs["c_kv_norm"][l], f32).reshape(4, 128).T for l in range(L)], axis=1))
    c["sink"] = np.ascontiguousarray(np.broadcast_to(np.asarray(inputs["a_sink"][:L], f32).reshape(1, L * 12), (128, L * 12)))
    c["ident"] = np.eye(128, dtype=f32)
    sel = np.zeros((128, 4), f32)
    sel[:, rank] = 1.0
    c["sel"] = sel
    edge = np.ones((128, 2), f32)
    if rank == 0:
        edge[:, 0] = 0.0
    if rank == 3:
        edge[:, 1] = 0.0
    c["edge"] = edge
    inv = (1.0 / (np.float32(10000.0) ** (np.arange(0, 64, 2, dtype=f32) / np.float32(64)))).astype(f32)
    pos = np.arange(rank * S, (rank + 1) * S, dtype=f32)
    ang = (pos[:, None] * inv[None, :]).astype(f32)
    cs, sn = np.cos(ang).astype(f32).T, np.sin(ang).astype(f32).T
    c["costab"] = np.ascontiguousarray(np.concatenate([cs, cs], 0))
    c["sintab"] = np.ascontiguousarray(np.concatenate([-sn, sn], 0))
    k = np.arange(128)[:, None]
    q = np.arange(128)[None, :]
    slopes = (2.0 ** (-8.0 * np.arange(1, 13, dtype=f32) / 12)).astype(f32)
    bA = np.zeros((128, 12, 3, 128), f32)
    for j in range(3):
        rel = (j - 1) * 128 + k - q
        for h in range(12):
            bA[:, h, j, :] = np.where(np.abs(rel) <= 128, -slopes[h] * np.abs(rel).astype(f32), NEG)
    c["biasA"] = np.ascontiguousarray(bA.reshape(128, 12 * 384))
    a = np.arange(2)[:, None, None, None, None]
    kc = np.arange(64)[None, :, None, None, None]
    o = np.arange(-3, 4)[None, None, :, None, None]
    b = np.arange(2)[None, None, None, :, None]
    qc = np.arange(64)[None, None, None, None, :]
    dr = 2 * o + a - b + 0 * kc + 0 * qc
    dc = kc - qc + 0 * a + 0 * o + 0 * b
    ok = (np.abs(dr) <= 7) & (np.abs(dc) <= 15)
    dri = np.clip(dr + 7, 0, 14)
    dci = np.clip(dc + 15, 0, 30)
    rpb = np.asarray(inputs["b_rpb"], f32)
    bT = np.zeros((L * 8, 128, 896), f32)
    for l in range(L):
        for h in range(8):
            bT[l * 8 + h] = np.where(ok, rpb[l, h][dri, dci], np.float32(0.0)).reshape(128, 896)
    c["bT"] = bT
    rows = SEQ // 64
    NBl = S // 128
    mk = np.zeros((5, 128, 896), f32)
    for cls, ml in enumerate((0, 1, 2, NBl - 2, NBl - 1)):
        m = rank * NBl + ml
        qr = 2 * m + b
        kr = 2 * (m + o) + a
        rs = np.clip(qr - 4, 0, rows - 8)
        cst = np.clip(qc - 8, 0, 48)
        valid = (kr >= rs) & (kr < rs + 8) & (kc >= cst) & (kc < cst + 16) & (kr >= 0) & (kr < rows)
        mk[cls] = valid.astype(f32).reshape(128, 896)
    c["maskB"] = np.ascontiguousarray(mk.transpose(1, 0, 2).reshape(128, 5 * 896)).astype(ml_dtypes.bfloat16)
    return c


_CACHE = {}


def run(inputs, S, L, cores, taps=(), stop=None):
    groups = tuple(tuple(range(g, g + 4)) for g in range(0, len(cores), 4))
    key = (S, L, repr(taps), stop, groups)
    if key not in _CACHE:
        _CACHE[key] = build(S, L, taps, stop, groups)[0]
    nc = _CACHE[key]
    wnames = ["w_in", "c_w_q_b", "c_w_kv_b", "w_out", "x_w_q", "x_w_k", "x_w_v", "x_w_o", "w_gate", "w_up", "w_down"]
    shared = {n: np.ascontiguousarray(np.asarray(inputs[n], np.float32)[:L]) for n in wnames}
    cst = [host_consts(S, L, inputs, r) for r in range(4)]
    in_maps = []
    for (b, r) in cores:
        m = dict(shared)
        m.update(cst[r])
        m["x"] = np.ascontiguousarray(np.asarray(inputs["x"], np.float32)[b, r * S:(r + 1) * S])
        m["mem"] = np.ascontiguousarray(np.asarray(inputs["mem"], np.float32)[b])
        in_maps.append(m)
    res = run_bass_kernel_spmd(nc, in_maps, core_ids=list(range(len(cores))))
    return res.results


def kernel(**inputs):
    cores = [(b, r) for b in range(2) for r in range(4)]
    res = run(inputs, 2048, 2, cores)
    out = np.empty((2, 8192, 4096), np.float32)
    for i, (b, r) in enumerate(cores):
        out[b, r * 2048:(r + 1) * 2048] = res[i]["out"]
    return out
```
